# Optimizing a Trainium2 kernel written in Bass

```python
import jax, jax.numpy as jnp
from jax import lax
import numpy as np

D_MODEL = 2048
BATCH = 4
SEQ = 4096
DEPTH = 2

CHUNK = 64
Q_BLOCK = 128
EPS = 1e-6

SSM_D_INNER = 2 * D_MODEL
SSM_HEAD_DIM = 64
SSM_HEADS = SSM_D_INNER // SSM_HEAD_DIM
SSM_GROUPS = 8
SSM_HEADS_PER_GROUP = SSM_HEADS // SSM_GROUPS
SSM_STATE = 128
SSM_CONV = 4
SSM_CONV_DIM = SSM_D_INNER + 2 * SSM_GROUPS * SSM_STATE

MLA_HEADS = 16
MLA_Q_RANK = 512
MLA_KV_RANK = 512
MLA_NOPE = 128
MLA_ROPE = 64
MLA_V = 128
MLA_QK = MLA_NOPE + MLA_ROPE
MLA_OUT = MLA_HEADS * MLA_V
ROPE_BASE = 10000.0

FFN_DIM = 5632
FFN_CONV = 3

IN_SIZES = (SSM_D_INNER,
            SSM_CONV_DIM,
            SSM_HEADS,
            MLA_Q_RANK,
            MLA_KV_RANK,
            MLA_ROPE,
            D_MODEL,
            D_MODEL)
IN_DIM = sum(IN_SIZES)

kernel_name = 'hybrid_ssd_mla_gated_block'


def rms_norm(x, w):
    xf = x.astype(jnp.float32)
    y = xf * lax.rsqrt(jnp.mean(xf * xf, axis=-1, keepdims=True) + EPS)
    return (y * w.astype(jnp.float32)).astype(x.dtype)


def modulate(h, shift, scale):
    return h * (1.0 + scale[:, None, :]) + shift[:, None, :]


def split_cols(t, sizes):
    idx, acc = [], 0
    for s in sizes[:-1]:
        acc += s
        idx.append(acc)
    return jnp.split(t, idx, axis=-1)


def causal_dwconv(x, w, b):
    k = w.shape[0]
    y = lax.conv_general_dilated(x, w[:, None, :].astype(x.dtype), window_strides=(1,),
                                 padding=[(k - 1, 0)], dimension_numbers=('NWC', 'WIO', 'NWC'),
                                 feature_group_count=x.shape[-1])
    return y + b.astype(x.dtype)


def apply_rope(x, cos, sin):
    half = x.shape[-1] // 2
    x1, x2 = x[..., :half], x[..., half:]
    cos = cos.astype(x.dtype)
    sin = sin.astype(x.dtype)
    return jnp.concatenate([x1 * cos - x2 * sin, x2 * cos + x1 * sin], axis=-1)


def ssd_chunked_scan(xs, dt, a, bm, cm):
    bsz, seq = xs.shape[0], xs.shape[1]
    nc = seq // CHUNK

    def to_chunks(t):
        return jnp.moveaxis(t.reshape((bsz, nc, CHUNK) + t.shape[2:]), 1, 0)

    causal = jnp.tril(jnp.ones((CHUNK, CHUNK), dtype=bool))[None, :, :, None, None]

    def step(state, inp):
        x_c, dt_c, b_c, c_c = inp
        acs = jnp.cumsum(dt_c * a, axis=1)
        seg = acs[:, :, None] - acs[:, None, :]
        lmask = jnp.exp(jnp.where(causal, seg, -jnp.inf))
        cb = jnp.einsum('bqgn,bkgn->bqkg', c_c, b_c)
        y_diag = jnp.einsum('bqkg,bqkgr,bkgr,bkgrp->bqgrp', cb, lmask, dt_c, x_c)
        y_off = jnp.einsum('bqgn,bgrpn->bqgrp', c_c, state) * jnp.exp(acs)[..., None]
        decay = jnp.exp(acs[:, -1:] - acs) * dt_c
        new_state = (state * jnp.exp(acs[:, -1])[..., None, None]
                     + jnp.einsum('bkgn,bkgr,bkgrp->bgrpn', b_c, decay, x_c))
        return new_state, y_diag + y_off

    state0 = jnp.zeros((bsz, SSM_GROUPS, SSM_HEADS_PER_GROUP, SSM_HEAD_DIM, SSM_STATE), jnp.float32)
    _, ys = lax.scan(step, state0, (to_chunks(xs), to_chunks(dt), to_chunks(bm), to_chunks(cm)))
    return jnp.moveaxis(ys, 0, 1).reshape(xs.shape)


def ssd_branch(z, xbc, dt_raw, conv_w, conv_b, dt_bias, a_log, d_skip, norm_w, w_out):
    bsz, seq, _ = z.shape
    f32 = jnp.float32
    xbc = jax.nn.silu(causal_dwconv(xbc, conv_w, conv_b))
    xs, bm, cm = split_cols(xbc, (SSM_D_INNER, SSM_GROUPS * SSM_STATE, SSM_GROUPS * SSM_STATE))
    xs = xs.astype(f32).reshape(bsz, seq, SSM_GROUPS, SSM_HEADS_PER_GROUP, SSM_HEAD_DIM)
    bm = bm.astype(f32).reshape(bsz, seq, SSM_GROUPS, SSM_STATE)
    cm = cm.astype(f32).reshape(bsz, seq, SSM_GROUPS, SSM_STATE)
    dt = jax.nn.softplus(dt_raw.astype(f32) + dt_bias.astype(f32))
    dt = dt.reshape(bsz, seq, SSM_GROUPS, SSM_HEADS_PER_GROUP)
    a = -jnp.exp(a_log.astype(f32)).reshape(SSM_GROUPS, SSM_HEADS_PER_GROUP)
    y = ssd_chunked_scan(xs, dt, a, bm, cm)
    y = y + xs * d_skip.astype(f32).reshape(SSM_GROUPS, SSM_HEADS_PER_GROUP)[..., None]
    y = y.reshape(bsz, seq, SSM_D_INNER).astype(z.dtype)
    y = rms_norm(y * jax.nn.silu(z), norm_w)
    return y @ w_out


def chunk_causal_attention(q, k, v):
    bsz, seq, nh, dk = q.shape
    nb = seq // Q_BLOCK
    qb = jnp.moveaxis(q.reshape(bsz, nb, Q_BLOCK, nh, dk), 1, 0)
    key_chunk = jnp.arange(seq) // CHUNK
    scale = dk ** -0.5

    def one_block(args):
        qi, i = args
        s = jnp.einsum('bqhd,bkhd->bhqk', qi, k).astype(jnp.float32) * scale
        q_chunk = (i * Q_BLOCK + jnp.arange(Q_BLOCK)) // CHUNK
        mask = key_chunk[None, :] <= q_chunk[:, None]
        p = jax.nn.softmax(jnp.where(mask, s, -jnp.inf), axis=-1).astype(v.dtype)
        return jnp.einsum('bhqk,bkhd->bqhd', p, v)

    out = lax.map(one_block, (qb, jnp.arange(nb)))
    return jnp.moveaxis(out, 0, 1).reshape(bsz, seq, nh, v.shape[-1])


def mla_branch(q_lat, kv_lat, k_rope, cos, sin, q_norm_w, w_q_up, kv_norm_w, w_kv_up,
               qn_w, kn_w, w_out):
    bsz, seq, _ = q_lat.shape
    q = (rms_norm(q_lat, q_norm_w) @ w_q_up).reshape(bsz, seq, MLA_HEADS, MLA_QK)
    kv = (rms_norm(kv_lat, kv_norm_w) @ w_kv_up).reshape(bsz, seq, MLA_HEADS, MLA_NOPE + MLA_V)
    k_nope, v = kv[..., :MLA_NOPE], kv[..., MLA_NOPE:]
    k = jnp.concatenate([k_nope, jnp.broadcast_to(k_rope[:, :, None, :],
                                                  (bsz, seq, MLA_HEADS, MLA_ROPE))], axis=-1)
    q = rms_norm(q, qn_w)
    k = rms_norm(k, kn_w)
    q = jnp.concatenate([q[..., :MLA_NOPE], apply_rope(q[..., MLA_NOPE:], cos, sin)], axis=-1)
    k = jnp.concatenate([k[..., :MLA_NOPE], apply_rope(k[..., MLA_NOPE:], cos, sin)], axis=-1)
    o = chunk_causal_attention(q, k, v)
    return o.reshape(bsz, seq, MLA_OUT) @ w_out


def conv_glu(h, w_up, conv_w, conv_b, w_down):
    u = h @ w_up
    gate, val = u[..., :FFN_DIM], u[..., FFN_DIM:]
    gate = causal_dwconv(gate, conv_w, conv_b)
    return (jax.nn.silu(gate) * val) @ w_down


def setup_inputs(seed: int = 0) -> dict:
    key = jax.random.key(seed)
    ks = jax.random.split(key, 32)
    f32 = jnp.float32

    def nrm(k, shape, scale):
        return jax.random.normal(k, shape, f32) * scale

    def gain(k, shape):
        return 1.0 + 0.02 * jax.random.normal(k, shape, f32)

    L = DEPTH
    x = jax.random.normal(ks[0], (BATCH, SEQ, D_MODEL), f32)
    c = jax.random.normal(ks[1], (BATCH, D_MODEL), f32)
    offset = jax.random.randint(ks[2], (BATCH, 1), 0, 64, dtype=jnp.int32) * CHUNK
    positions = offset + jnp.arange(SEQ, dtype=jnp.int32)[None, :]
    dt_init = jnp.exp(jax.random.uniform(ks[3], (L, SSM_HEADS), f32, np.log(1e-3), np.log(1e-1)))
    ssm_dt_bias = dt_init + jnp.log(-jnp.expm1(-dt_init))
    ssm_a_log = jnp.log(jax.random.uniform(ks[4], (L, SSM_HEADS), f32, 1.0, 16.0))
    return {
        'x': x,
        'c': c,
        'positions': positions,
        'norm1_w': gain(ks[5], (L, D_MODEL)),
        'norm2_w': gain(ks[6], (L, D_MODEL)),
        'w_mod': nrm(ks[7], (L, D_MODEL, 6 * D_MODEL), 0.5 * D_MODEL ** -0.5),
        'b_mod': nrm(ks[8], (L, 6 * D_MODEL), 0.02),
        'w_in': nrm(ks[9], (L, D_MODEL, IN_DIM), D_MODEL ** -0.5),
        'ssm_conv_w': nrm(ks[10], (L, SSM_CONV, SSM_CONV_DIM), SSM_CONV ** -0.5),
        'ssm_conv_b': nrm(ks[11], (L, SSM_CONV_DIM), 0.02),
        'ssm_dt_bias': ssm_dt_bias,
        'ssm_a_log': ssm_a_log,
        'ssm_d': gain(ks[12], (L, SSM_HEADS)),
        'ssm_norm_w': gain(ks[13], (L, SSM_D_INNER)),
        'w_ssm_out': nrm(ks[14], (L, SSM_D_INNER, D_MODEL), SSM_D_INNER ** -0.5),
        'mla_q_norm_w': gain(ks[15], (L, MLA_Q_RANK)),
        'w_q_up': nrm(ks[16], (L, MLA_Q_RANK, MLA_HEADS * MLA_QK), MLA_Q_RANK ** -0.5),
        'mla_kv_norm_w': gain(ks[17], (L, MLA_KV_RANK)),
        'w_kv_up': nrm(ks[18], (L, MLA_KV_RANK, MLA_HEADS * (MLA_NOPE + MLA_V)), MLA_KV_RANK ** -0.5),
        'qk_norm_q_w': gain(ks[19], (L, MLA_QK)),
        'qk_norm_k_w': gain(ks[20], (L, MLA_QK)),
        'w_mla_out': nrm(ks[21], (L, MLA_OUT, D_MODEL), MLA_OUT ** -0.5),
        'w_mix_out': nrm(ks[22], (L, D_MODEL, D_MODEL), D_MODEL ** -0.5),
        'w_ffn_up': nrm(ks[23], (L, D_MODEL, 2 * FFN_DIM), D_MODEL ** -0.5),
        'ffn_conv_w': nrm(ks[24], (L, FFN_CONV, FFN_DIM), FFN_CONV ** -0.5),
        'ffn_conv_b': nrm(ks[25], (L, FFN_DIM), 0.02),
        'w_ffn_down': nrm(ks[26], (L, FFN_DIM, D_MODEL), FFN_DIM ** -0.5),
    }


def reference(x, c, positions, norm1_w, norm2_w, w_mod, b_mod, w_in, ssm_conv_w, ssm_conv_b,
              ssm_dt_bias, ssm_a_log, ssm_d, ssm_norm_w, w_ssm_out, mla_q_norm_w, w_q_up,
              mla_kv_norm_w, w_kv_up, qk_norm_q_w, qk_norm_k_w, w_mla_out, w_mix_out,
              w_ffn_up, ffn_conv_w, ffn_conv_b, w_ffn_down):
    inv_freq = ROPE_BASE ** (-jnp.arange(0, MLA_ROPE, 2, dtype=jnp.float32) / MLA_ROPE)
    ang = positions.astype(jnp.float32)[..., None] * inv_freq
    cos = jnp.cos(ang)[:, :, None, :]
    sin = jnp.sin(ang)[:, :, None, :]
    cond = jax.nn.silu(c)
    for l in range(DEPTH):
        mod = cond @ w_mod[l] + b_mod[l]
        shift1, scale1, gate1, shift2, scale2, gate2 = jnp.split(mod, 6, axis=-1)

        h = modulate(rms_norm(x, norm1_w[l]), shift1, scale1)
        z, xbc, dt_raw, q_lat, kv_lat, k_rope, ga, gb = split_cols(h @ w_in[l], IN_SIZES)
        y_ssd = ssd_branch(z, xbc, dt_raw, ssm_conv_w[l], ssm_conv_b[l], ssm_dt_bias[l],
                           ssm_a_log[l], ssm_d[l], ssm_norm_w[l], w_ssm_out[l])
        y_mla = mla_branch(q_lat, kv_lat, k_rope, cos, sin, mla_q_norm_w[l], w_q_up[l],
                           mla_kv_norm_w[l], w_kv_up[l], qk_norm_q_w[l], qk_norm_k_w[l],
                           w_mla_out[l])
        merged = jax.nn.sigmoid(ga) * y_ssd + jax.nn.sigmoid(gb) * y_mla
        x = x + gate1[:, None, :] * (merged @ w_mix_out[l])

        h = modulate(rms_norm(x, norm2_w[l]), shift2, scale2)
        x = x + gate2[:, None, :] * conv_glu(h, w_ffn_up[l], ffn_conv_w[l], ffn_conv_b[l],
                                             w_ffn_down[l])
    return x
```

```python
import numpy as np
from contextlib import ExitStack
import concourse.bass as bass
import concourse.mybir as mybir
from concourse.bass_utils import run_bass_kernel_spmd

F32 = mybir.dt.float32
BF16 = mybir.dt.bfloat16
I32 = mybir.dt.int32
AF = mybir.ActivationFunctionType
ALU = mybir.AluOpType
AX = mybir.AxisListType

SAME_ENGINE_SYNC = ('pool', 'act')

D = 2048
KD = 16
DI = 4096
NH = 64
NG = 8
FF = 5632
KF = 44
L = 2
EPS = 1e-6
PI = float(np.pi)


class Buf:
    __slots__ = ("t", "name", "w", "r", "sem", "psum")

    def __init__(self, t, name, psum=False):
        self.psum = psum
        self.t = t
        self.name = name
        self.w = None
        self.r = {}
        self.sem = None

    def __getitem__(self, k):
        return self.t[k]


class EngS:
    def __init__(self, name, eng, sem):
        self.name = name
        self.eng = eng
        self.sem = sem
        self.count = 0
        self.waited = {}


class Ring:
    def __init__(self, bufs):
        self.bufs = bufs
        self.i = 0

    def next(self):
        b = self.bufs[self.i % len(self.bufs)]
        self.i += 1
        return b

    def prev(self):
        return self.bufs[(self.i - 2) % len(self.bufs)]


class KB:
    def __init__(self, nc, n_dma_sems=80):
        self.nc = nc
        self.root = ExitStack()
        self.E = {}
        for name, eng in (("pe", nc.tensor), ("act", nc.scalar), ("dve", nc.vector),
                          ("pool", nc.gpsimd), ("sp", nc.sync)):
            sem = self.root.enter_context(nc.semaphore("sem_" + name))
            self.E[name] = EngS(name, eng, sem)
        self.dsems = []
        for i in range(n_dma_sems):
            self.dsems.append([self.root.enter_context(nc.semaphore("dsem%d" % i)), 0])
        self.free_dsems = list(range(n_dma_sems))
        self.stacks = [self.root]
        self.phase_bufs = [[]]
        self.uid = 0
        self.dummy = Buf(None, "dummy")
        self.phase_bufs[0].append(self.dummy)

    def push(self):
        self.stacks.append(ExitStack())
        self.phase_bufs.append([])

    def pop(self):
        self.barrier()
        for b in self.phase_bufs.pop():
            if b.sem is not None:
                self.free_dsems.append(b.sem)
                b.sem = None
        self.stacks.pop().close()

    def tile(self, name, shape, dtype, space="sbuf"):
        es = self.stacks[-1]
        self.uid += 1
        uname = "%s_%d" % (name, self.uid)
        if space == "sbuf":
            t = es.enter_context(self.nc.sbuf_tensor(uname, list(shape), dtype))
        else:
            t = es.enter_context(self.nc.psum_tensor(uname, list(shape), dtype))
        b = Buf(t, uname, space != "sbuf")
        self.phase_bufs[-1].append(b)
        return b

    def pool(self, name, shape, dtype, n, space="sbuf"):
        return Ring([self.tile("%s%d" % (name, i), shape, dtype, space) for i in range(n)])

    def _wait(self, E, ev):
        if ev is None:
            return
        if ev[0] == "e":
            _, en, cnt, tiny = ev
            if en == E.name and en not in SAME_ENGINE_SYNC and not (tiny and en != "pe"):
                return
            if E.waited.get(en, 0) >= cnt:
                return
            E.eng.wait_ge(self.E[en].sem, cnt)
            E.waited[en] = cnt
        else:
            slot = ev[1]
            h, tot = self.dsems[slot]
            key = ("d", slot)
            if E.waited.get(key, 0) >= tot:
                return
            E.eng.wait_ge(h, tot)
            E.waited[key] = tot

    def _deps(self, E, reads, writes):
        for b in reads:
            self._wait(E, b.w)
            if b.psum:
                for en2, ev in list(b.r.items()):
                    if en2 != E.name:
                        self._wait(E, ev)
        for b in writes:
            self._wait(E, b.w)
            for ev in list(b.r.values()):
                self._wait(E, ev)

    def op(self, en, fn, reads=(), writes=(), tiny=False):
        E = self.E[en]
        self._deps(E, reads, writes)
        inst = fn(E.eng)
        E.count += 1
        inst.then_inc(E.sem, 1)
        ev = ("e", en, E.count, tiny)
        for b in reads:
            b.r[en] = ev
        for b in writes:
            b.w = ev
            b.r = {}
        return inst

    def dma(self, qn, out, in_, sb, load, first=True, **kw):
        E = self.E[qn]
        if first:
            if load:
                self._deps(E, (), [sb])
            else:
                self._deps(E, [sb], ())
        if sb.sem is None:
            sb.sem = self.free_dsems.pop()
        slot = self.dsems[sb.sem]
        inst = E.eng.dma_start(out=out, in_=in_, **kw)
        inst.then_inc(slot[0], 16)
        slot[1] += 16
        ev = ("d", sb.sem)
        if load:
            sb.w = ev
            sb.r = {}
        else:
            sb.r["dma"] = ev
        return inst

    def barrier(self):
        used = [i for i in range(len(self.dsems)) if self.dsems[i][1] > 0]
        for E in self.E.values():
            for E2 in self.E.values():
                if E2 is E or E2.count == 0:
                    continue
                if E.waited.get(E2.name, 0) < E2.count:
                    E.eng.wait_ge(E2.sem, E2.count)
                    E.waited[E2.name] = E2.count
            for i in used:
                h, tot = self.dsems[i]
                key = ("d", i)
                if E.waited.get(key, 0) < tot:
                    E.eng.wait_ge(h, tot)
                    E.waited[key] = tot
        for bl in self.phase_bufs:
            for b in bl:
                b.w = None
                b.r = {}


V_BMOD = 0
V_N1 = 96
V_N2 = 112
V_CW = 128
V_CB = 320
V_SNW = 368
V_FW = 400
V_FB = 532
NV = 576
R_DTB = 0
R_ALOG = 64
R_D = 128
R_QNW = 192
R_KVNW = 704
R_QKQ = 1216
R_QKK = 1408
NR = 1600


class _Stop(Exception):
    pass


def build(T, dbg=(), stop=None):
    NT = T // 128
    NTB = T // 512
    nc = bass.Bass("TRN2", target_bir_lowering=False)

    def din(name, shape, dt=F32):
        return nc.dram_tensor(name, list(shape), dt, kind="ExternalInput").ap()

    def dscr(name, shape, dt):
        kind = "ExternalOutput" if name in dbg else "Internal"
        return nc.dram_tensor(name, list(shape), dt, kind=kind).ap()

    xT_in = din("xT", [D, T])
    cT_in = din("cT", [128, KD])
    pos_in = din("pos", [128, NT], I32)
    invf_in = din("invf", [32])
    ident_in = din("ident", [128, 128])
    triu_in = din("triu", [128, 128])
    mneg_in = din("mneg", [128, 128])
    dmask_in = din("dmask", [128, 128])
    wmod_in = din("wmod", [L, KD, 128, 6 * D])
    vecs_in = din("vecs", [L, 128, NV])
    rv_in = din("rv", [L, NR])
    wa_in = din("wa", [L, 80, 128, KD, 128])
    wb_in = din("wb", [L, 10, 128, KD, 512])
    wb2_in = din("wb2", [L, 128, KD, 128])
    wqup_in = din("wqup", [L, 128, 4, 3072])
    wkvup_in = din("wkvup", [L, 128, 4, 4096])
    wsso_in = din("wsso", [L, 16, 128, 32, 128])
    wmo_in = din("wmo", [L, 16, 128, 16, 128])
    wmix_in = din("wmix", [L, 16, 128, 16, 128])
    wup_in = din("wup", [L, 88, 128, KD, 128])
    wdn_in = din("wdn", [L, 16, 128, KF, 128])
    xT = nc.dram_tensor("outT", [D, T], F32, kind="ExternalOutput").ap()

    hT = dscr("hT", [D, T], BF16)
    zs_tm = dscr("zs_tm", [T, DI], BF16)
    lat_tm = dscr("lat_tm", [T, 1088], BF16)
    dt_tm = dscr("dt_tm", [T, 64], F32)
    x_tm = dscr("x_tm", [T, DI], BF16)
    B_tm = dscr("B_tm", [T, 1024], BF16)
    BT = dscr("BT", [1024, T], BF16)
    CT = dscr("CT", [1024, T], BF16)
    sga = dscr("sga", [D, T], BF16)
    sgb = dscr("sgb", [D, T], BF16)
    acsB = dscr("acsB", [NT, 64 * 128], F32)
    acs2 = dscr("acs2", [T, 128], F32)
    qTn = dscr("qTn", [16, 128, T], BF16)
    qTr = dscr("qTr", [16, 64, T], BF16)
    kTn = dscr("kTn", [16, 128, T], BF16)
    kTr = dscr("kTr", [16, 64, T], BF16)
    V_tm = dscr("V_tm", [16, 128, T], BF16)
    oT = dscr("oT", [D, T], BF16)
    gT = dscr("gT", [DI, T], BF16)
    mergedT = dscr("mergedT", [D, T], BF16)
    actT = dscr("actT", [FF, T], BF16)

    kb = KB(nc)
    op = kb.op

    def chk(k):
        if stop == k:
            kb.barrier()
            raise _Stop()

    try:
     with kb.root:
        identf = kb.tile("identf", [128, 128], F32)
        identb = kb.tile("identb", [128, 128], BF16)
        onesb = kb.tile("onesb", [128, 128], BF16)
        triu = kb.tile("triu", [128, 128], F32)
        mneg = kb.tile("mneg", [128, 128], F32)
        dmask = kb.tile("dmask", [128, 128], BF16)
        cs = kb.tile("cs", [128, NT, 64], F32)
        condT = kb.tile("condT", [128, KD, 2], F32)
        epsc = kb.tile("epsc", [128, 1], F32)

        kb.push()
        tmpf = kb.tile("tmpf", [128, 128], F32)
        kb.dma("sp", identf[:], ident_in[:, :], identf, True)
        kb.dma("sp", triu[:], triu_in[:, :], triu, True)
        kb.dma("sp", mneg[:], mneg_in[:, :], mneg, True)
        kb.dma("sp", tmpf[:], dmask_in[:, :], tmpf, True)
        op("dve", lambda e: e.tensor_copy(out=dmask[:], in_=tmpf[:]), [tmpf], [dmask], tiny=True)
        op("dve", lambda e: e.tensor_copy(out=identb[:], in_=identf[:]), [identf], [identb], tiny=True)
        op("pool", lambda e: e.memset(onesb[:], 1.0), (), [onesb])
        op("pool", lambda e: e.memset(epsc[:], EPS), (), [epsc])
        ct = kb.tile("ct", [128, KD], F32)
        kb.dma("sp", ct[:], cT_in[:, :], ct, True)
        op("act", lambda e: e.activation(out=condT[:, :, 0], in_=ct[:], func=AF.Silu), [ct], [condT])
        op("act", lambda e: e.activation(out=condT[:, :, 1], in_=ct[:], func=AF.Silu), [ct], [condT])
        posi = kb.tile("posi", [128, NT], I32)
        posf = kb.tile("posf", [128, NT], F32)
        invf = kb.tile("invf", [128, 32], F32)
        ang = kb.tile("ang", [128, NT, 32], F32)
        a2 = kb.tile("a2", [128, NT, 32], F32)
        uu = kb.tile("uu", [128, NT, 32], F32)
        ki = kb.tile("ki", [128, NT, 32], I32)
        kf = kb.tile("kf", [128, NT, 32], F32)
        kb.dma("sp", posi[:], pos_in[:, :], posi, True)
        kb.dma("sp", invf[:], invf_in.partition_broadcast(128), invf, True)
        op("dve", lambda e: e.tensor_copy(out=posf[:], in_=posi[:]), [posi], [posf], tiny=True)
        op("dve", lambda e: e.tensor_tensor(out=ang[:], in0=posf[:].unsqueeze(2).to_broadcast([128, NT, 32]),
                                            in1=invf[:].unsqueeze(1).to_broadcast([128, NT, 32]), op=ALU.mult),
           [posf, invf], [ang], tiny=True)
        C1 = 6.28125
        C2 = float(2 * np.pi - 6.28125)
        for off, lo in ((PI / 2, 0), (0.0, 32)):
            op("dve", lambda e, off=off: e.tensor_scalar(out=a2[:], in0=ang[:], scalar1=off, scalar2=None, op0=ALU.add), [ang], [a2], tiny=True)
            op("dve", lambda e: e.tensor_scalar(out=uu[:], in0=a2[:], scalar1=float(1 / (2 * np.pi)), scalar2=None, op0=ALU.mult), [a2], [uu], tiny=True)
            op("dve", lambda e: e.tensor_copy(out=ki[:], in_=uu[:]), [uu], [ki], tiny=True)
            op("dve", lambda e: e.tensor_copy(out=kf[:], in_=ki[:]), [ki], [kf], tiny=True)
            op("dve", lambda e: e.scalar_tensor_tensor(out=a2[:], in0=kf[:], scalar=-C1, in1=a2[:], op0=ALU.mult, op1=ALU.add), [kf, a2], [a2], tiny=True)
            op("dve", lambda e: e.scalar_tensor_tensor(out=a2[:], in0=kf[:], scalar=-C2, in1=a2[:], op0=ALU.mult, op1=ALU.add), [kf, a2], [a2], tiny=True)
            op("dve", lambda e: e.tensor_scalar(out=uu[:], in0=a2[:], scalar1=PI, scalar2=-2 * PI, op0=ALU.is_gt, op1=ALU.mult), [a2], [uu], tiny=True)
            op("dve", lambda e: e.tensor_tensor(out=a2[:], in0=a2[:], in1=uu[:], op=ALU.add), [a2, uu], [a2], tiny=True)
            op("dve", lambda e: e.tensor_scalar(out=uu[:], in0=a2[:], scalar1=-PI, scalar2=2 * PI, op0=ALU.is_lt, op1=ALU.mult), [a2], [uu], tiny=True)
            op("dve", lambda e: e.tensor_tensor(out=a2[:], in0=a2[:], in1=uu[:], op=ALU.add), [a2, uu], [a2], tiny=True)
            op("dve", lambda e: e.tensor_scalar(out=a2[:], in0=a2[:], scalar1=-PI, scalar2=PI, op0=ALU.max, op1=ALU.min), [a2], [a2], tiny=True)
            op("act", lambda e, lo=lo: e.activation(out=cs[:, :, lo:lo + 32], in_=a2[:], func=AF.Sin), [a2], [cs])
        kb.dma("sp", xT[:, :], xT_in[:, :], kb.dummy, True)
        kb.pop()

        chk('setup')
        def gemm_fm(TBLK, acts, jobs, epi, pre=None):
            TBLK = min(TBLK, T)
            nstream = max(len(j) for j in jobs)
            abufs = [kb.tile("act%d" % i, [128, KC, TBLK], BF16) for i, (_, KC) in enumerate(acts)]
            wkc = [0] * nstream
            for j in jobs:
                for si, (ai, wd, M) in enumerate(j):
                    wkc[si] = max(wkc[si], acts[ai][1])
            wpools = [kb.pool("w%d" % si, [128, wkc[si], 128], BF16, 3) for si in range(nstream)]
            pspools = [kb.pool("ps%d" % si, [128, 512], F32, 3, "psum") for si in range(nstream)]

            def loadw(job):
                res = []
                for si, (ai, wd, M) in enumerate(job):
                    KC = acts[ai][1]
                    w = wpools[si].next()
                    for k0 in range(0, KC, 16):
                        k1 = min(KC, k0 + 16)
                        kb.dma("pool", w[:, k0:k1, :M], wd[:, k0:k1, :], w, True, first=(k0 == 0))
                    res.append(w)
                return res

            pending = None
            for blk in range(T // TBLK):
                for ai, (ad, KC) in enumerate(acts):
                    src = ad.rearrange("(kc p) t -> p kc t", p=128)
                    for k0 in range(0, KC, 4):
                        kb.dma("sp", abufs[ai][:, k0:k0 + 4, :], src[:, k0:k0 + 4, blk * TBLK:(blk + 1) * TBLK],
                               abufs[ai], True, first=(k0 == 0))
                nxt = loadw(jobs[0])
                for ji, job in enumerate(jobs):
                    cur = nxt
                    if ji + 1 < len(jobs):
                        nxt = loadw(jobs[ji + 1])
                    for sb in range(TBLK // 512):
                        pss = []
                        for si, (ai, wd, M) in enumerate(job):
                            KC = acts[ai][1]
                            ps = pspools[si].next()
                            w = cur[si]
                            a = abufs[ai]
                            for kc in range(KC):
                                op("pe", lambda e, ps=ps, w=w, a=a, kc=kc, M=M, KC=KC, sb=sb: e.matmul(
                                    ps[:M, :], lhsT=w[:, kc, :M], rhs=a[:, kc, sb * 512:(sb + 1) * 512],
                                    start=(kc == 0), stop=(kc == KC - 1)), [w, a], [ps])
                            pss.append(ps)
                        pctx = pre(ji, blk * TBLK + sb * 512) if pre is not None else None
                        if pending is not None:
                            epi(*pending)
                        pending = (ji, blk * TBLK + sb * 512, pss, pctx)
            if pending is not None:
                epi(*pending)

        def gemm_tm(ad, blocks, epi):
            a = kb.tile("acttm", [128, KD, T], BF16)
            src = ad.rearrange("(kc p) t -> p kc t", p=128)
            for k0 in range(0, KD, 4):
                kb.dma("sp", a[:, k0:k0 + 4, :], src[:, k0:k0 + 4, :], a, True, first=(k0 == 0))
            wpool = kb.pool("wtm", [128, KD, 512], BF16, 2)
            pspool = kb.pool("pstm", [128, 512], F32, 3, "psum")

            def loadw(bi):
                wd, N = blocks[bi]
                w = wpool.next()
                for k0 in range(0, KD, 4):
                    kb.dma("pool", w[:, k0:k0 + 4, :N], wd[:, k0:k0 + 4, :], w, True, first=(k0 == 0))
                return w
            nxt = loadw(0)
            for bi, (wd, N) in enumerate(blocks):
                w = nxt
                if bi + 1 < len(blocks):
                    nxt = loadw(bi + 1)
                for tt in range(NT):
                    ps = pspool.next()
                    for kc in range(KD):
                        op("pe", lambda e, ps=ps, w=w, kc=kc, N=N, tt=tt: e.matmul(
                            ps[:, :N], lhsT=a[:, kc, tt * 128:(tt + 1) * 128], rhs=w[:, kc, :N],
                            start=(kc == 0), stop=(kc == KD - 1)), [w, a], [ps])
                    epi(bi, tt, ps)

        def norm_phase(Acol, Bcol, mvb):
            kb.push()
            xbp = kb.pool("xb", [128, KD, 512], F32, 2)
            sq = kb.tile("sq", [128, KD, 512], BF16)
            hbp = kb.pool("hb", [128, KD, 512], BF16, 2)
            tmp = kb.tile("tmp", [128, KD, 512], F32)
            rsp = kb.pool("rstd", [128, 512], F32, 2)
            psp = kb.pool("psn", [128, 512], F32, 2, "psum")
            src = xT.rearrange("(kc p) t -> p kc t", p=128)
            dst = hT.rearrange("(kc p) t -> p kc t", p=128)
            for blk in range(NTB):
                sl = slice(blk * 512, (blk + 1) * 512)
                xb = xbp.next()
                for k0 in range(0, KD, 4):
                    kb.dma("sp", xb[:, k0:k0 + 4, :], src[:, k0:k0 + 4, sl], xb, True, first=(k0 == 0))
                op("act", lambda e, xb=xb: e.activation(out=sq[:], in_=xb[:], func=AF.Square), [xb], [sq])
                ps = psp.next()
                for kc in range(KD):
                    op("pe", lambda e, ps=ps, kc=kc: e.matmul(ps[:], lhsT=onesb[:], rhs=sq[:, kc, :],
                                                              start=(kc == 0), stop=(kc == KD - 1)), [onesb, sq], [ps])
                rs = rsp.next()
                op("act", lambda e, rs=rs, ps=ps: e.activation(out=rs[:], in_=ps[:], func=AF.Ln, bias=epsc[:], scale=1.0 / D), [ps, epsc], [rs])
                op("act", lambda e, rs=rs: e.activation(out=rs[:], in_=rs[:], func=AF.Exp, scale=-0.5), [rs], [rs])
                op("dve", lambda e, xb=xb, rs=rs: e.tensor_tensor(out=tmp[:], in0=xb[:], in1=rs[:].unsqueeze(1).to_broadcast([128, KD, 512]), op=ALU.mult), [xb, rs], [tmp])
                hb = hbp.next()
                for kc in range(KD):
                    en = "act" if kc % 2 == 0 else "dve"
                    if en == "act":
                        op("act", lambda e, hb=hb, kc=kc: e.activation(out=hb[:, kc, :], in_=tmp[:, kc, :], func=AF.Identity,
                                                                       scale=Acol(kc), bias=Bcol(kc)), [tmp, mvb], [hb])
                    else:
                        op("dve", lambda e, hb=hb, kc=kc: e.tensor_scalar(out=hb[:, kc, :], in0=tmp[:, kc, :], scalar1=Acol(kc),
                                                                         scalar2=Bcol(kc), op0=ALU.mult, op1=ALU.add), [tmp, mvb], [hb])
                for k0 in range(0, KD, 4):
                    kb.dma("sp", dst[:, k0:k0 + 4, sl], hb[:, k0:k0 + 4, :], hb, False, first=(k0 == 0))
            kb.pop()

        for l in range(L):
            kb.push()
            vecs = kb.tile("vecs", [128, NV], F32)
            rv = kb.tile("rv", [128, NR], F32)
            mv = kb.tile("mv", [128, 6, KD], F32)
            aneg = kb.tile("aneg", [128, 64], F32)
            kb.dma("sp", vecs[:], vecs_in[l], vecs, True)
            kb.dma("sp", rv[:], rv_in[l].partition_broadcast(128), rv, True)
            op("act", lambda e: e.activation(out=aneg[:], in_=rv[:, R_ALOG:R_ALOG + 64], func=AF.Exp), [rv], [aneg], tiny=True)
            op("dve", lambda e: e.tensor_scalar(out=aneg[:], in0=aneg[:], scalar1=-1.0, scalar2=None, op0=ALU.mult), [aneg], [aneg], tiny=True)

            kb.push()
            wmp = kb.pool("wm", [128, 6 * D], F32, 2)
            acc = kb.tile("macc", [128, 192], F32)
            modv = kb.tile("modv", [128, 96], F32)
            psm = kb.pool("psm", [128, 512], F32, 2, "psum")
            for kc in range(KD):
                w = wmp.next()
                for q in range(4):
                    kb.dma("sp", w[:, q * 3072:(q + 1) * 3072], wmod_in[l, kc][:, q * 3072:(q + 1) * 3072], w, True, first=(q == 0))
                ps = psm.next()
                for m in range(96):
                    op("pe", lambda e, ps=ps, w=w, m=m, kc=kc: e.matmul(ps[:, 2 * m:2 * m + 2], lhsT=w[:, m * 128:(m + 1) * 128],
                                                                      rhs=condT[:, kc, :], start=True, stop=True), [w, condT], [ps])
                if kc == 0:
                    op("dve", lambda e, ps=ps: e.tensor_copy(out=acc[:], in_=ps[:, 0:192]), [ps], [acc], tiny=True)
                else:
                    op("dve", lambda e, ps=ps: e.tensor_tensor(out=acc[:], in0=acc[:], in1=ps[:, 0:192], op=ALU.add), [ps, acc], [acc], tiny=True)
            op("dve", lambda e: e.tensor_tensor(out=modv[:], in0=acc[:].rearrange("p (m two) -> p m two", two=2)[:, :, 0],
                                                in1=vecs[:, V_BMOD:V_BMOD + 96], op=ALU.add), [acc, vecs], [modv], tiny=True)
            op("dve", lambda e: e.scalar_tensor_tensor(out=mv[:, 0, :], in0=modv[:, 16:32], scalar=1.0, in1=vecs[:, V_N1:V_N1 + 16], op0=ALU.add, op1=ALU.mult), [modv, vecs], [mv], tiny=True)
            op("dve", lambda e: e.tensor_copy(out=mv[:, 1, :], in_=modv[:, 0:16]), [modv], [mv], tiny=True)
            op("dve", lambda e: e.tensor_copy(out=mv[:, 2, :], in_=modv[:, 32:48]), [modv], [mv], tiny=True)
            op("dve", lambda e: e.scalar_tensor_tensor(out=mv[:, 3, :], in0=modv[:, 64:80], scalar=1.0, in1=vecs[:, V_N2:V_N2 + 16], op0=ALU.add, op1=ALU.mult), [modv, vecs], [mv], tiny=True)
            op("dve", lambda e: e.tensor_copy(out=mv[:, 4, :], in_=modv[:, 48:64]), [modv], [mv], tiny=True)
            op("dve", lambda e: e.tensor_copy(out=mv[:, 5, :], in_=modv[:, 80:96]), [modv], [mv], tiny=True)
            kb.pop()

            chk('p1')
            norm_phase(lambda kc: mv[:, 0, kc:kc + 1], lambda kc: mv[:, 1, kc:kc + 1], mv)

            chk('p2')
            kb.push()
            cbp = kb.pool("cb", [128, 515], F32, 2)
            accp = kb.pool("cacc", [128, 512], F32, 2)
            up = kb.pool("u", [128, 512], BF16, 3)
            utp = kb.pool("utm", [128, 4, 128], BF16, 2)
            pstp = kb.pool("pst", [128, 8, 128], BF16, 1, "psum")

            def epi_a(ji, t0, pss, pctx=None):
                ps = pss[0]
                sl = slice(t0, t0 + 512)
                if ji < 48:
                    cb = cbp.next()
                    if t0 == 0:
                        op("pool", lambda e: e.memset(cb[:, 0:3], 0.0), (), [cb])
                    else:
                        pv = cbp.prev()
                        op("pool", lambda e: e.tensor_copy(out=cb[:, 0:3], in_=pv[:, 512:515]), [pv], [cb])
                    op("act", lambda e: e.activation(out=cb[:, 3:515], in_=ps[:], func=AF.Copy), [ps], [cb])
                    ac = accp.next()
                    cw = lambda k: vecs[:, V_CW + k * 48 + ji:V_CW + k * 48 + ji + 1]
                    op("dve", lambda e: e.tensor_scalar(out=ac[:], in0=cb[:, 0:512], scalar1=cw(0), scalar2=vecs[:, V_CB + ji:V_CB + ji + 1],
                                                        op0=ALU.mult, op1=ALU.add), [cb, vecs], [ac])
                    for k in (1, 2, 3):
                        op("dve", lambda e, k=k: e.scalar_tensor_tensor(out=ac[:], in0=cb[:, k:k + 512], scalar=cw(k), in1=ac[:],
                                                                        op0=ALU.mult, op1=ALU.add), [cb, vecs, ac], [ac])
                    u = up.next()
                    op("act", lambda e: e.activation(out=u[:], in_=ac[:], func=AF.Silu), [ac], [u])
                    if ji >= 32:
                        dstT = BT if ji < 40 else CT
                        r0 = (ji - 32) * 128 if ji < 40 else (ji - 40) * 128
                        kb.dma("sp", dstT[r0:r0 + 128, sl], u[:], u, False)
                    if ji < 40:
                        pst = pstp.next()
                        for j in range(4):
                            op("pe", lambda e, j=j: e.transpose(out=pst[:, j, :], in_=u[:, j * 128:(j + 1) * 128], identity=identb[:]), [u, identb], [pst])
                        ut = utp.next()
                        op("dve", lambda e: e.tensor_copy(out=ut[:], in_=pst[:, 0:4, :]), [pst], [ut])
                        if ji < 32:
                            dd = x_tm[t0:t0 + 512, ji * 128:(ji + 1) * 128]
                        else:
                            dd = B_tm[t0:t0 + 512, (ji - 32) * 128:(ji - 31) * 128]
                        kb.dma("sp", dd.rearrange("(j p) c -> p j c", p=128), ut[:], ut, False)
                else:
                    u = up.next()
                    op("act", lambda e: e.activation(out=u[:], in_=ps[:], func=AF.Sigmoid), [ps], [u])
                    dstT = sga if ji < 64 else sgb
                    r0 = (ji - 48) * 128 if ji < 64 else (ji - 64) * 128
                    kb.dma("sp", dstT[r0:r0 + 128, sl], u[:], u, False)

            gemm_fm(T, [(hT, KD)], [[(0, wa_in[l, i], 128)] for i in range(80)], epi_a)
            kb.pop()

            chk('p3a')
            kb.push()
            ztp = kb.pool("zt", [128, 512], BF16, 3)
            dxp = kb.pool("dtx", [128, 64], F32, 2)
            dtp = kb.pool("dtt", [128, 64], F32, 2)
            dop = kb.pool("dto", [128, 64], F32, 2)

            def epi_b(bi, tt, ps):
                rows = slice(tt * 128, (tt + 1) * 128)
                zt = ztp.next()
                if bi < 8:
                    op("act", lambda e: e.activation(out=zt[:], in_=ps[:], func=AF.Silu), [ps], [zt])
                    kb.dma("sp", zs_tm[rows, bi * 512:(bi + 1) * 512], zt[:], zt, False)
                elif bi < 10:
                    op("act", lambda e: e.activation(out=zt[:], in_=ps[:], func=AF.Copy), [ps], [zt])
                    kb.dma("sp", lat_tm[rows, (bi - 8) * 512:(bi - 7) * 512], zt[:], zt, False)
                else:
                    op("act", lambda e: e.activation(out=zt[:, 0:64], in_=ps[:, 64:128], func=AF.Copy), [ps], [zt])
                    kb.dma("sp", lat_tm[rows, 1024:1088], zt[:, 0:64], zt, False)
                    dx = dxp.next()
                    dt_ = dtp.next()
                    do = dop.next()
                    op("dve", lambda e: e.tensor_tensor(out=dx[:], in0=ps[:, 0:64], in1=rv[:, R_DTB:R_DTB + 64], op=ALU.add), [ps, rv], [dx], tiny=True)
                    op("dve", lambda e: e.tensor_scalar(out=dt_[:], in0=dx[:], scalar1=30.0, scalar2=None, op0=ALU.min), [dx], [dt_], tiny=True)
                    op("act", lambda e: e.activation(out=dt_[:], in_=dt_[:], func=AF.Exp), [dt_], [dt_], tiny=True)
                    op("act", lambda e: e.activation(out=dt_[:], in_=dt_[:], func=AF.Ln, bias=1.0), [dt_], [dt_], tiny=True)
                    op("dve", lambda e: e.tensor_tensor(out=do[:], in0=dx[:], in1=dt_[:], op=ALU.max), [dx, dt_], [do], tiny=True)
                    kb.dma("sp", dt_tm[rows, :], do[:], do, False)

            blocks = [(wb_in[l, i], 512) for i in range(10)] + [(wb2_in[l], 128)]
            gemm_tm(hT, blocks, epi_b)
            kb.pop()

            chk('p3b')
            kb.push()
            dlp = kb.pool("dl", [128, 64], F32, 2)
            dap = kb.pool("da", [128, 64], F32, 2)
            lnp = kb.pool("ln", [128, 64], F32, 2)
            a2p = kb.pool("a2", [128, 128], F32, 2)
            atp = kb.pool("at", [64, 128], F32, 2)
            psa = kb.pool("psa", [128, 512], F32, 2, "psum")
            pstf = kb.pool("pstf", [128, 512], F32, 2, "psum")
            for c in range(NT):
                rows = slice(c * 128, (c + 1) * 128)
                dl = dlp.next()
                kb.dma("sp", dl[:], dt_tm[rows, :], dl, True)
                da = dap.next()
                op("dve", lambda e: e.tensor_tensor(out=da[:], in0=dl[:], in1=aneg[:], op=ALU.mult), [dl, aneg], [da], tiny=True)
                ps = psa.next()
                op("pe", lambda e: e.matmul(ps[:, 0:64], lhsT=triu[:], rhs=da[:], start=True, stop=True), [triu, da], [ps])
                ln_ = lnp.next()
                op("act", lambda e: e.activation(out=ln_[:], in_=dl[:], func=AF.Ln), [dl], [ln_])
                a2_ = a2p.next()
                op("dve", lambda e: e.tensor_copy(out=a2_[:, 64:128], in_=ps[:, 0:64]), [ps], [a2_], tiny=True)
                op("dve", lambda e: e.tensor_tensor(out=a2_[:, 0:64], in0=a2_[:, 64:128], in1=ln_[:], op=ALU.subtract), [a2_, ln_], [a2_], tiny=True)
                kb.dma("sp", acs2[rows, :], a2_[:], a2_, False)
                pt = pstf.next()
                op("pe", lambda e: e.transpose(out=pt[:64, 0:128], in_=a2_[:, 64:128], identity=identf[:]), [a2_, identf], [pt])
                at = atp.next()
                op("dve", lambda e: e.tensor_copy(out=at[:], in_=pt[:64, 0:128]), [pt], [at], tiny=True)
                kb.dma("sp", acsB[c].rearrange("(r q) -> r q", q=128), at[:], at, False)
            kb.pop()

            chk('p3c')
            kb.push()
            wq = kb.tile("wq", [128, 4, 3072], BF16)
            wkv = kb.tile("wkv", [128, 4, 4096], BF16)
            for kc in range(4):
                for c0 in range(0, 3072, 1024):
                    kb.dma("pool", wq[:, kc, c0:c0 + 1024], wqup_in[l][:, kc, c0:c0 + 1024], wq, True, first=(kc == 0 and c0 == 0))
                for c0 in range(0, 4096, 1024):
                    kb.dma("pool", wkv[:, kc, c0:c0 + 1024], wkvup_in[l][:, kc, c0:c0 + 1024], wkv, True, first=(kc == 0 and c0 == 0))
            latp = kb.pool("lat", [128, 1088], BF16, 2)
            junk = kb.tile("junk", [128, 3072], F32)
            junka = kb.tile("junka", [128, 512], F32)
            sm = kb.pool("sm", [128, 64], F32, 2)
            lnb = kb.pool("lnb", [128, 512], BF16, 2)
            lnT = kb.pool("lnT", [128, 4, 128], BF16, 2)
            hf = kb.pool("hf", [128, 16, 192], F32, 4)
            hbq = kb.pool("hbq", [128, 16, 192], BF16, 2)
            rt = kb.pool("rt", [128, 4, 16, 32], F32, 1)
            vbp = kb.pool("vb", [128, 16, 128], BF16, 2)
            tns = kb.pool("tns", [128, 16, 128], BF16, 2)
            trs = kb.pool("trs", [64, 16, 128], BF16, 2)
            psg = kb.pool("psg", [128, 512], F32, 4, "psum")
            pstb = kb.pool("pstb", [128, 8, 128], BF16, 2, "psum")

            def rstd_small(dst, src, n):
                op("act", lambda e: e.activation(out=dst, in_=src, func=AF.Ln, bias=epsc[:], scale=1.0 / n), [s_, epsc], [s_], tiny=True)
                op("act", lambda e: e.activation(out=dst, in_=dst, func=AF.Exp, scale=-0.5), [s_], [s_], tiny=True)

            def latent_norm_T(lat, c0, woff, s_):
                op("act", lambda e: e.activation(out=junka[:], in_=lat[:, c0:c0 + 512], func=AF.Square), [lat], [junka])
                op("dve", lambda e: e.tensor_reduce(out=s_[:, 0:1], in_=junka[:], axis=AX.X, op=ALU.add), [junka], [s_], tiny=True)
                op("act", lambda e: e.activation(out=s_[:, 1:2], in_=s_[:, 0:1], func=AF.Ln, bias=epsc[:], scale=1.0 / 512), [s_, epsc], [s_], tiny=True)
                op("act", lambda e: e.activation(out=s_[:, 1:2], in_=s_[:, 1:2], func=AF.Exp, scale=-0.5), [s_], [s_], tiny=True)
                lb = lnb.next()
                op("dve", lambda e: e.scalar_tensor_tensor(out=lb[:], in0=lat[:, c0:c0 + 512], scalar=s_[:, 1:2], in1=rv[:, woff:woff + 512],
                                                           op0=ALU.mult, op1=ALU.mult), [lat, s_, rv], [lb])
                pt = pstb.next()
                for j in range(4):
                    op("pe", lambda e, j=j: e.transpose(out=pt[:, j, :], in_=lb[:, j * 128:(j + 1) * 128], identity=identb[:]), [lb, identb], [pt])
                lt = lnT.next()
                op("dve", lambda e: e.tensor_copy(out=lt[:], in_=pt[:, 0:4, :]), [pt], [lt])
                return lt

            def head_post(h_, woff, scale, tt, dTn, dTr, s_):
                op("act", lambda e: e.activation(out=junk[:], in_=h_[:].rearrange("p h d -> p (h d)"), func=AF.Square), [h_], [junk])
                op("dve", lambda e: e.tensor_reduce(out=s_[:, 16:32], in_=junk[:].rearrange("p (h d) -> p h d", d=192), axis=AX.X, op=ALU.add), [junk], [s_], tiny=True)
                op("act", lambda e: e.activation(out=s_[:, 32:48], in_=s_[:, 16:32], func=AF.Ln, bias=epsc[:], scale=1.0 / 192), [s_, epsc], [s_], tiny=True)
                op("act", lambda e: e.activation(out=s_[:, 32:48], in_=s_[:, 32:48], func=AF.Exp, scale=-0.5), [s_], [s_], tiny=True)
                if scale != 1.0:
                    op("dve", lambda e: e.tensor_scalar(out=s_[:, 32:48], in0=s_[:, 32:48], scalar1=scale, scalar2=None, op0=ALU.mult), [s_], [s_], tiny=True)
                op("dve", lambda e: e.tensor_tensor(out=h_[:], in0=h_[:], in1=s_[:, 32:48].unsqueeze(2).to_broadcast([128, 16, 192]), op=ALU.mult), [h_, s_], [h_])
                op("pool", lambda e: e.tensor_tensor(out=h_[:], in0=h_[:], in1=rv[:, woff:woff + 192].unsqueeze(1).to_broadcast([128, 16, 192]), op=ALU.mult), [h_, rv], [h_])
                chk('pm2h1')
                hb_ = hbq.next()
                op("act", lambda e: e.activation(out=hb_[:, :, 0:128], in_=h_[:, :, 0:128], func=AF.Copy), [h_], [hb_])
                r_ = rt.next()
                x1 = h_[:, :, 128:160]
                x2 = h_[:, :, 160:192]
                cosb = cs[:, tt, 0:32].unsqueeze(1).to_broadcast([128, 16, 32])
                sinb = cs[:, tt, 32:64].unsqueeze(1).to_broadcast([128, 16, 32])
                op("dve", lambda e: e.tensor_tensor(out=r_[:, 0], in0=x1, in1=cosb, op=ALU.mult), [h_, cs], [r_])
                op("dve", lambda e: e.tensor_tensor(out=r_[:, 1], in0=x2, in1=sinb, op=ALU.mult), [h_, cs], [r_])
                op("pool", lambda e: e.tensor_tensor(out=r_[:, 2], in0=x2, in1=cosb, op=ALU.mult), [h_, cs], [r_])
                op("pool", lambda e: e.tensor_tensor(out=r_[:, 3], in0=x1, in1=sinb, op=ALU.mult), [h_, cs], [r_])
                op("dve", lambda e: e.tensor_tensor(out=hb_[:, :, 128:160], in0=r_[:, 0], in1=r_[:, 1], op=ALU.subtract), [r_], [hb_])
                op("dve", lambda e: e.tensor_tensor(out=hb_[:, :, 160:192], in0=r_[:, 2], in1=r_[:, 3], op=ALU.add), [r_], [hb_])
                chk('pm2h2')
                tn = tns.next()
                tr = trs.next()
                for h0 in (0, 8):
                    pt = pstb.next()
                    for j in range(8):
                        op("pe", lambda e, j=j: e.transpose(out=pt[:, j, :], in_=hb_[:, h0 + j, 0:128], identity=identb[:]), [hb_, identb], [pt])
                    op("act" if h0 == 0 else "dve", lambda e: (e.activation(out=tn[:, h0:h0 + 8, :], in_=pt[:], func=AF.Copy) if h0 == 0 else
                                                              e.tensor_copy(out=tn[:, h0:h0 + 8, :], in_=pt[:])), [pt], [tn])
                    pt2 = pstb.next()
                    for j in range(8):
                        op("pe", lambda e, j=j: e.transpose(out=pt2[:64, j, :], in_=hb_[:, h0 + j, 128:192], identity=identb[:]), [hb_, identb], [pt2])
                    op("dve", lambda e: e.tensor_copy(out=tr[:, h0:h0 + 8, :], in_=pt2[:64, :, :]), [pt2], [tr])
                chk('pm2h3')
                tsl = slice(tt * 128, (tt + 1) * 128)
                for h0 in range(0, 16, 4):
                    kb.dma("sp", dTn[h0:h0 + 4, :, tsl].rearrange("h d t -> d h t"), tn[:, h0:h0 + 4, :], tn, False, first=(h0 == 0))
                for h0 in range(0, 16, 8):
                    kb.dma("sp", dTr[h0:h0 + 8, :, tsl].rearrange("h d t -> d h t"), tr[:, h0:h0 + 8, :], tr, False, first=(h0 == 0))

            def pm2_a(tt):
                rows = slice(tt * 128, (tt + 1) * 128)
                lat = latp.next()
                kb.dma("sp", lat[:], lat_tm[rows, :], lat, True)
                s_ = sm.next()
                lt = latent_norm_T(lat, 0, R_QNW, s_)
                qf = hf.next()
                for cbk in range(6):
                    ps = psg.next()
                    for kc in range(4):
                        op("pe", lambda e, ps=ps, kc=kc, cbk=cbk: e.matmul(ps[:], lhsT=lt[:, kc, :], rhs=wq[:, kc, cbk * 512:(cbk + 1) * 512],
                                                                         start=(kc == 0), stop=(kc == 3)), [lt, wq], [ps])
                    op("act" if cbk % 2 == 0 else "dve",
                       lambda e, ps=ps, cbk=cbk: (e.activation(out=qf[:].rearrange("p h d -> p (h d)")[:, cbk * 512:(cbk + 1) * 512], in_=ps[:], func=AF.Copy)
                                                  if cbk % 2 == 0 else
                                                  e.tensor_copy(out=qf[:].rearrange("p h d -> p (h d)")[:, cbk * 512:(cbk + 1) * 512], in_=ps[:])), [ps], [qf])
                yield
                lt2 = latent_norm_T(lat, 512, R_KVNW, s_)
                kf_ = hf.next()
                vb = vbp.next()
                for cbk in range(8):
                    ps = psg.next()
                    for kc in range(4):
                        op("pe", lambda e, ps=ps, kc=kc, cbk=cbk: e.matmul(ps[:], lhsT=lt2[:, kc, :], rhs=wkv[:, kc, cbk * 512:(cbk + 1) * 512],
                                                                         start=(kc == 0), stop=(kc == 3)), [lt2, wkv], [ps])
                    psv = ps[:].rearrange("p (h c) -> p h c", c=256)
                    op("act", lambda e, psv=psv, cbk=cbk: e.activation(out=kf_[:, 2 * cbk:2 * cbk + 2, 0:128], in_=psv[:, :, 0:128], func=AF.Copy), [ps], [kf_])
                    op("dve", lambda e, psv=psv, cbk=cbk: e.tensor_copy(out=vb[:, 2 * cbk:2 * cbk + 2, :], in_=psv[:, :, 128:256]), [ps], [vb])
                op("dve", lambda e: e.tensor_copy(out=kf_[:, :, 128:192], in_=lat[:, 1024:1088].unsqueeze(1).to_broadcast([128, 16, 64])), [lat], [kf_])
                for h0 in range(0, 16, 4):
                    kb.dma("sp", V_tm[h0:h0 + 4, :, tt * 128:(tt + 1) * 128].rearrange("h p c -> p h c"), vb[:, h0:h0 + 4, :], vb, False, first=(h0 == 0))
                res_a[tt] = (qf, kf_, s_)
                yield

            def pm2_b(tt, ctx):
                qf, kf_, s_ = ctx
                head_post(qf, R_QKQ, float(192 ** -0.5), tt, qTn, qTr, s_)
                yield
                head_post(kf_, R_QKK, 1.0, tt, kTn, kTr, s_)
                yield

            res_a = {}
            ga = pm2_a(0)
            for _ in ga:
                pass
            for tt in range(NT):
                ga = pm2_a(tt + 1) if tt + 1 < NT else iter(())
                gb = pm2_b(tt, res_a[tt])
                next(ga, None)
                next(gb, None)
                for _ in ga:
                    pass
                for _ in gb:
                    pass
            kb.pop()

            chk('pm2')
            kb.push()
            ktn = kb.pool("ktn", [128, T], BF16, 2)
            ktr = kb.pool("ktr", [128, T], BF16, 2)
            qtn = kb.pool("qtn", [128, T], BF16, 2)
            qtr = kb.pool("qtr", [128, T], BF16, 2)
            vh = kb.pool("vh", [128, NT, 128], BF16, 2)
            ptp = kb.pool("pT", [128, 512], BF16, 3)
            recp = kb.pool("rec", [128, 512], F32, 2)
            obp = kb.pool("ob", [128, 512], BF16, 2)
            pssc = kb.pool("pssc", [128, 512], F32, 4, "psum")
            pso = kb.pool("pso", [128, 512], F32, 2, "psum")
            pss_ = kb.pool("pssum", [128, 512], F32, 2, "psum")
            for b_ in ktr.bufs + qtr.bufs:
                op("pool", lambda e, b_=b_: e.memset(b_[64:128, :], 0.0), (), [b_])
            for h in range(16):
                kn = ktn.next(); kr = ktr.next(); qn = qtn.next(); qr = qtr.next(); v_ = vh.next()
                kb.dma("sp", kn[:], kTn[h], kn, True)
                kb.dma("sp", kr[0:64, :], kTr[h], kr, True)
                kb.dma("sp", qn[:], qTn[h], qn, True)
                kb.dma("sp", qr[0:64, :], qTr[h], qr, True)
                kb.dma("sp", v_[:].rearrange("p j c -> p (j c)"), V_tm[h], v_, True)
                for I in range(NTB):
                    po = pso.next()
                    pm = pss_.next()
                    nj = 4 * I + 4
                    def scores(j):
                        a = j - 4 * I if j >= 4 * I else 0
                        c0 = 128 * a
                        q0 = I * 512 + c0
                        q1 = (I + 1) * 512
                        sc = pssc.next()
                        op("pe", lambda e: e.matmul(sc[:, c0:512], lhsT=kn[:, j * 128:(j + 1) * 128], rhs=qn[:, q0:q1], start=True, stop=False), [kn, qn], [sc])
                        op("pe", lambda e: e.matmul(sc[:, c0:512], lhsT=kr[:, j * 128:(j + 1) * 128], rhs=qr[:, q0:q1], start=False, stop=True), [kr, qr], [sc])
                        return sc, c0

                    def rest(j, sc, c0):
                        pT = ptp.next()
                        op("act", lambda e: e.activation(out=pT[:, c0:512], in_=sc[:, c0:512], func=AF.Exp), [sc], [pT])
                        if j >= 4 * I:
                            op("dve", lambda e: e.tensor_tensor(out=pT[:, c0:c0 + 128], in0=pT[:, c0:c0 + 128], in1=dmask[:], op=ALU.mult), [pT, dmask], [pT])
                        op("pe", lambda e: e.matmul(po[:, c0:512], lhsT=v_[:, j, :], rhs=pT[:, c0:512], start=(j == 0), stop=(j == nj - 1)), [v_, pT], [po])
                        op("pe", lambda e: e.matmul(pm[:, c0:512], lhsT=onesb[:], rhs=pT[:, c0:512], start=(j == 0), stop=(j == nj - 1)), [onesb, pT], [pm])

                    pend = [scores(0)]
                    if nj > 1:
                        pend.append(scores(1))
                    for j in range(nj):
                        if j + 2 < nj:
                            pend.append(scores(j + 2))
                        rest(j, *pend.pop(0))
                    rec = recp.next()
                    op("dve", lambda e: e.reciprocal(out=rec[:], in_=pm[:]), [pm], [rec])
                    ob = obp.next()
                    op("dve", lambda e: e.tensor_tensor(out=ob[:], in0=po[:], in1=rec[:], op=ALU.mult), [po, rec], [ob])
                    kb.dma("sp", oT[h * 128:(h + 1) * 128, I * 512:(I + 1) * 512], ob[:], ob, False)
            kb.pop()

            chk('pm3')
            kb.push()
            S = [kb.tile("S%d" % g, [128, 512], F32) for g in range(NG)]
            Sb = [kb.tile("Sb%d" % g, [128, 512], BF16) for g in range(NG)]
            for g in range(NG):
                op("pool", lambda e, g=g: e.memset(S[g][:], 0.0), (), [S[g]])
                op("pool", lambda e, g=g: e.memset(Sb[g][:], 0.0), (), [Sb[g]])
            xcp = kb.pool("xc", [128, DI], BF16, 2)
            zcp = kb.pool("zc", [128, DI], BF16, 1)
            btmp = kb.pool("btm", [128, 1024], BF16, 2)
            btp = kb.pool("bt", [128, NG, 128], BF16, 2)
            ctp = kb.pool("ct", [128, NG, 128], BF16, 2)
            a2p = kb.pool("a2s", [128, 128], F32, 2)
            abg = [kb.tile("ab%d" % g, [128, 8, 128], F32) for g in range(NG)]
            Eg = [[kb.tile("E%d_%d" % (i, g), [128, 8, 128], BF16) for g in range(NG)] for i in range(2)]
            xdg = [[kb.tile("xd%d_%d" % (i, g), [128, 512], BF16) for g in range(NG)] for i in range(2)]
            cbp2 = kb.pool("cbb", [128, NG, 128], BF16, 1)
            elp = kb.pool("el", [128, 64], F32, 2)
            eap = kb.pool("ea", [128, 64], F32, 2)
            ytp = kb.pool("yt", [128, 512], F32, 2)
            y2p = kb.pool("y2", [128, 512], F32, 2)
            sqp = kb.pool("sq", [128, 512], F32, 1)
            gqg = [kb.tile("gq%d" % g, [128, 512], F32) for g in range(NG)]
            gbp = kb.pool("gb", [128, DI], BF16, 1)
            ssp = kb.pool("ssg", [128, 16], F32, 2)
            gtp = kb.pool("gts", [128, 32, 128], BF16, 1)
            pscb = kb.pool("pscb", [128, 4, 128], F32, 2, "psum")
            psyd = kb.pool("psyd", [128, 512], F32, 2, "psum")
            psyo = kb.pool("psyo", [128, 512], F32, 2, "psum")
            psst = kb.pool("psst", [128, 512], F32, 1, "psum")
            pstr = kb.pool("pstr", [128, 8, 128], BF16, 1, "psum")

            def ssd_front(c):
                rows = slice(c * 128, (c + 1) * 128)
                par = c % 2
                xc = xcp.next(); btm = btmp.next(); bt = btp.next(); ct_ = ctp.next(); a2_ = a2p.next()
                kb.dma("sp", a2_[:], acs2[rows, :], a2_, True)
                for g0 in (0, 4):
                    kb.dma("sp", bt[:, g0:g0 + 4, :], BT[g0 * 128:(g0 + 4) * 128, rows].rearrange("(g n) k -> n g k", n=128), bt, True, first=(g0 == 0))
                    kb.dma("sp", ct_[:, g0:g0 + 4, :], CT[g0 * 128:(g0 + 4) * 128, rows].rearrange("(g n) k -> n g k", n=128), ct_, True, first=(g0 == 0))
                kb.dma("sp", xc[:], x_tm[rows, :], xc, True)
                kb.dma("sp", btm[:], B_tm[rows, :], btm, True)
                el = elp.next(); ea = eap.next()
                op("act", lambda e: e.activation(out=ea[:], in_=a2_[:, 64:128], func=AF.Exp), [a2_], [ea])
                cbb = cbp2.next()
                for half in range(2):
                    pc = pscb.next()
                    for gg in range(4):
                        g = half * 4 + gg
                        op("pe", lambda e, pc=pc, g=g, gg=gg: e.matmul(pc[:, gg, :], lhsT=bt[:, g, :], rhs=ct_[:, g, :], start=True, stop=True), [bt, ct_], [pc])
                    op("act", lambda e, pc=pc, half=half: e.activation(out=cbb[:, half * 4:half * 4 + 4, :], in_=pc[:], func=AF.Copy), [pc], [cbb])
                for g in range(NG):
                    ab = abg[g]
                    kb.dma("sp", ab[:].rearrange("p r q -> p (r q)"), acsB[c, g * 1024:(g + 1) * 1024].partition_broadcast(128), ab, True)
                for g in range(NG):
                    ab = abg[g]
                    hs = slice(g * 8, (g + 1) * 8)
                    op("act", lambda e, ab=ab, hs=hs: e.activation(out=el[:, hs], in_=ab[:, :, 127], func=AF.Exp), [ab], [el], tiny=True)
                    op("dve", lambda e, ab=ab, hs=hs: e.tensor_tensor(out=ab[:], in0=ab[:], in1=a2_[:, hs].unsqueeze(2).to_broadcast([128, 8, 128]), op=ALU.subtract), [ab, a2_], [ab])
                for g in range(NG):
                    ab = abg[g]
                    op("dve" if g % 2 == 0 else "pool", lambda e, ab=ab: e.tensor_tensor(out=ab[:], in0=ab[:], in1=mneg[:].unsqueeze(1).to_broadcast([128, 8, 128]), op=ALU.add), [ab, mneg], [ab])
                for g in range(NG):
                    ab = abg[g]
                    E_ = Eg[par][g]
                    op("act", lambda e, ab=ab, E_=E_: e.activation(out=E_[:], in_=ab[:], func=AF.Exp), [ab], [E_])
                for g in range(NG):
                    E_ = Eg[par][g]
                    xd = xdg[par][g]
                    op("pool", lambda e, E_=E_, xd=xd, g=g: e.tensor_tensor(out=xd[:].rearrange("p (r d) -> p r d", d=64), in0=xc[:, g * 512:(g + 1) * 512].rearrange("p (r d) -> p r d", d=64),
                                                                        in1=E_[:, :, 127].unsqueeze(2).to_broadcast([128, 8, 64]), op=ALU.mult), [xc, E_], [xd])
                for g in range(NG):
                    E_ = Eg[par][g]
                    op("dve", lambda e, E_=E_, g=g: e.tensor_tensor(out=E_[:], in0=E_[:], in1=cbb[:, g, :].unsqueeze(1).to_broadcast([128, 8, 128]), op=ALU.mult), [E_, cbb], [E_])
                return (rows, par, xc, btm, ct_, ea, el)

            def ssd_back(c, ctx):
                rows, par, xc, btm, ct_, ea, el = ctx
                zc = zcp.next()
                kb.dma("sp", zc[:], zs_tm[rows, :], zc, True)
                ssg = ssp.next()
                stA = {}

                def st_a(g):
                    E_ = Eg[par][g]
                    pyd = psyd.next()
                    for r in range(8):
                        hh = g * 8 + r
                        op("pe", lambda e, pyd=pyd, hh=hh, r=r, E_=E_: e.matmul(pyd[:, r * 64:(r + 1) * 64], lhsT=E_[:, r, :], rhs=xc[:, hh * 64:(hh + 1) * 64],
                                                                                start=True, stop=True), [E_, xc], [pyd])
                    pyo = psyo.next()
                    op("pe", lambda e, pyo=pyo, g=g: e.matmul(pyo[:], lhsT=ct_[:, g, :], rhs=Sb[g][:], start=True, stop=True), [ct_, Sb[g]], [pyo])
                    yt = ytp.next()
                    op("dve", lambda e, pyo=pyo, g=g, yt=yt: e.tensor_tensor(out=yt[:].rearrange("p (r d) -> p r d", d=64), in0=pyo[:].rearrange("p (r d) -> p r d", d=64),
                                                                             in1=ea[:, g * 8:(g + 1) * 8].unsqueeze(2).to_broadcast([128, 8, 64]), op=ALU.mult), [pyo, ea], [yt])
                    op("dve", lambda e, pyd=pyd, yt=yt: e.tensor_tensor(out=yt[:], in0=yt[:], in1=pyd[:], op=ALU.add), [pyd, yt], [yt])
                    stA[g] = yt

                st_a(0)
                for g in range(NG):
                    gs = slice(g * 512, (g + 1) * 512)
                    xd = xdg[par][g]
                    gq = gqg[g]
                    if g + 1 < NG:
                        st_a(g + 1)
                    yt = stA[g]
                    y2 = y2p.next()
                    op("pool", lambda e, g=g, y2=y2, gs=gs: e.tensor_tensor(out=y2[:].rearrange("p (r d) -> p r d", d=64), in0=xc[:, gs].rearrange("p (r d) -> p r d", d=64),
                                                                            in1=rv[:, R_D + g * 8:R_D + (g + 1) * 8].unsqueeze(2).to_broadcast([128, 8, 64]), op=ALU.mult), [xc, rv], [y2])
                    op("pool", lambda e, y2=y2, yt=yt: e.tensor_tensor(out=y2[:], in0=y2[:], in1=yt[:], op=ALU.add), [y2, yt], [y2])
                    op("pool", lambda e, y2=y2, gs=gs, gq=gq: e.tensor_tensor(out=gq[:], in0=y2[:], in1=zc[:, gs], op=ALU.mult), [y2, zc], [gq])
                    sq_ = sqp.next()
                    op("act", lambda e, sq_=sq_, gq=gq: e.activation(out=sq_[:], in_=gq[:], func=AF.Square), [gq], [sq_])
                    pst_ = psst.next()
                    op("pe", lambda e, pst_=pst_, g=g, xd=xd: e.matmul(pst_[:], lhsT=btm[:, g * 128:(g + 1) * 128], rhs=xd[:], start=True, stop=True), [btm, xd], [pst_])
                    op("dve", lambda e, g=g: e.tensor_tensor(out=S[g][:].rearrange("p (r d) -> p r d", d=64), in0=S[g][:].rearrange("p (r d) -> p r d", d=64),
                                                             in1=el[:, g * 8:(g + 1) * 8].unsqueeze(2).to_broadcast([128, 8, 64]), op=ALU.mult), [S[g], el], [S[g]])
                    op("dve", lambda e, g=g, pst_=pst_: e.tensor_tensor(out=S[g][:], in0=S[g][:], in1=pst_[:], op=ALU.add), [S[g], pst_], [S[g]])
                    op("act", lambda e, g=g: e.activation(out=Sb[g][:], in_=S[g][:], func=AF.Copy), [S[g]], [Sb[g]])
                    op("dve", lambda e, sq_=sq_, g=g: e.tensor_reduce(out=ssg[:, g:g + 1], in_=sq_[:], axis=AX.X, op=ALU.add), [sq_], [ssg], tiny=True)
                op("dve", lambda e: e.tensor_reduce(out=ssg[:, 8:9], in_=ssg[:, 0:8], axis=AX.X, op=ALU.add), [ssg], [ssg], tiny=True)
                op("act", lambda e: e.activation(out=ssg[:, 9:10], in_=ssg[:, 8:9], func=AF.Ln, bias=epsc[:], scale=1.0 / DI), [ssg, epsc], [ssg], tiny=True)
                op("act", lambda e: e.activation(out=ssg[:, 9:10], in_=ssg[:, 9:10], func=AF.Exp, scale=-0.5), [ssg], [ssg], tiny=True)
                gb_ = gbp.next()
                for g in range(NG):
                    op("act", lambda e, g=g: e.activation(out=gb_[:, g * 512:(g + 1) * 512], in_=gqg[g][:], func=AF.Copy, scale=ssg[:, 9:10]), [gqg[g], ssg], [gb_])
                gts = gtp.next()
                for j0 in range(0, 32, 8):
                    pt = pstr.next()
                    for j in range(8):
                        op("pe", lambda e, pt=pt, j=j, j0=j0: e.transpose(out=pt[:, j, :], in_=gb_[:, (j0 + j) * 128:(j0 + j + 1) * 128], identity=identb[:]), [gb_, identb], [pt])
                    op("dve", lambda e, pt=pt, j0=j0: e.tensor_tensor(out=gts[:, j0:j0 + 8, :], in0=pt[:],
                                                                      in1=vecs[:, V_SNW + j0:V_SNW + j0 + 8].unsqueeze(2).to_broadcast([128, 8, 128]), op=ALU.mult), [pt, vecs], [gts])
                for j0 in range(0, 32, 4):
                    kb.dma("sp", gT[j0 * 128:(j0 + 4) * 128, rows].rearrange("(j p) t -> p j t", p=128), gts[:, j0:j0 + 4, :], gts, False, first=(j0 == 0))

            ctx = ssd_front(0)
            for c in range(NT):
                nxt = ssd_front(c + 1) if c + 1 < NT else None
                ssd_back(c, ctx)
                ctx = nxt
            kb.pop()

            chk('p4')
            kb.push()
            gap = kb.pool("ga", [128, 512], BF16, 3)
            gbp2 = kb.pool("gb2", [128, 512], BF16, 3)
            m1p = kb.pool("m1", [128, 512], F32, 2)
            m2p = kb.pool("m2", [128, 512], F32, 2)
            mbp = kb.pool("mb", [128, 512], BF16, 2)

            def pre5(ji, t0):
                sl = slice(t0, t0 + 512)
                rs_ = slice(ji * 128, (ji + 1) * 128)
                ga_ = gap.next(); gb3 = gbp2.next()
                kb.dma("sp", ga_[:], sga[rs_, sl], ga_, True)
                kb.dma("sp", gb3[:], sgb[rs_, sl], gb3, True)
                return ga_, gb3

            def epi5(ji, t0, pss, pctx):
                sl = slice(t0, t0 + 512)
                rs_ = slice(ji * 128, (ji + 1) * 128)
                ga_, gb3 = pctx
                m1 = m1p.next(); m2 = m2p.next(); mb = mbp.next()
                op("dve", lambda e: e.tensor_tensor(out=m1[:], in0=pss[0][:], in1=ga_[:], op=ALU.mult), [pss[0], ga_], [m1])
                op("dve", lambda e: e.tensor_tensor(out=m2[:], in0=pss[1][:], in1=gb3[:], op=ALU.mult), [pss[1], gb3], [m2])
                op("pool", lambda e: e.tensor_tensor(out=mb[:], in0=m1[:], in1=m2[:], op=ALU.add), [m1, m2], [mb])
                kb.dma("sp", mergedT[rs_, sl], mb[:], mb, False)

            gemm_fm(1024, [(gT, 32), (oT, KD)], [[(0, wsso_in[l, i], 128), (1, wmo_in[l, i], 128)] for i in range(16)], epi5, pre5)
            kb.pop()

            chk('p5')
            def resid_phase(act_d, KC, w_in_, gsel, TBLK):
                kb.push()
                xrp = kb.pool("xr", [128, 512], F32, 4)

                def pre6(ji, t0):
                    xr = xrp.next()
                    kb.dma("sp", xr[:], xT[ji * 128:(ji + 1) * 128, t0:t0 + 512], xr, True)
                    return xr

                def epi6(ji, t0, pss, xr):
                    sl = slice(t0, t0 + 512)
                    rs_ = slice(ji * 128, (ji + 1) * 128)
                    op("dve", lambda e: e.scalar_tensor_tensor(out=xr[:], in0=pss[0][:], scalar=mv[:, gsel, ji:ji + 1], in1=xr[:], op0=ALU.mult, op1=ALU.add),
                       [pss[0], mv, xr], [xr])
                    kb.dma("sp", xT[rs_, sl], xr[:], xr, False)

                gemm_fm(TBLK, [(act_d, KC)], [[(0, w_in_[l, i], 128)] for i in range(16)], epi6, pre6)
                kb.pop()

            resid_phase(mergedT, KD, wmix_in, 2, T)

            chk('p6')
            norm_phase(lambda kc: mv[:, 3, kc:kc + 1], lambda kc: mv[:, 4, kc:kc + 1], mv)

            chk('p7')
            kb.push()
            fcb = kb.pool("fcb", [128, 514], F32, 2)
            facc = kb.pool("facc", [128, 512], F32, 2)
            fsg = kb.pool("fsg", [128, 512], F32, 2)
            fab = kb.pool("fab", [128, 512], BF16, 3)

            def epi8(ji, t0, pss, pctx=None):
                sl = slice(t0, t0 + 512)
                cb = fcb.next()
                if t0 == 0:
                    op("pool", lambda e: e.memset(cb[:, 0:2], 0.0), (), [cb])
                else:
                    pv = fcb.prev()
                    op("pool", lambda e: e.tensor_copy(out=cb[:, 0:2], in_=pv[:, 512:514]), [pv], [cb])
                op("act", lambda e: e.activation(out=cb[:, 2:514], in_=pss[0][:], func=AF.Copy), [pss[0]], [cb])
                ac = facc.next()
                fw = lambda k: vecs[:, V_FW + k * 44 + ji:V_FW + k * 44 + ji + 1]
                op("dve", lambda e: e.tensor_scalar(out=ac[:], in0=cb[:, 0:512], scalar1=fw(0), scalar2=vecs[:, V_FB + ji:V_FB + ji + 1],
                                                    op0=ALU.mult, op1=ALU.add), [cb, vecs], [ac])
                for k in (1, 2):
                    op("dve", lambda e, k=k: e.scalar_tensor_tensor(out=ac[:], in0=cb[:, k:k + 512], scalar=fw(k), in1=ac[:], op0=ALU.mult, op1=ALU.add), [cb, vecs, ac], [ac])
                sg = fsg.next()
                op("act", lambda e: e.activation(out=sg[:], in_=ac[:], func=AF.Silu), [ac], [sg])
                ab_ = fab.next()
                op("dve", lambda e: e.tensor_tensor(out=ab_[:], in0=sg[:], in1=pss[1][:], op=ALU.mult), [sg, pss[1]], [ab_])
                kb.dma("sp", actT[ji * 128:(ji + 1) * 128, sl], ab_[:], ab_, False)

            gemm_fm(T, [(hT, KD)], [[(0, wup_in[l, i], 128), (0, wup_in[l, 44 + i], 128)] for i in range(44)], epi8)
            kb.pop()

            chk('p8')
            resid_phase(actT, KF, wdn_in, 5, 1024)

            kb.pop()
    except _Stop:
        pass
    return nc


def _tile_fm(w, KC):
    K, M = w.shape
    return np.ascontiguousarray(w.reshape(KC, 128, M // 128, 128).transpose(2, 1, 0, 3))


def _tile_tm(w, KC, N):
    K, M = w.shape
    return np.ascontiguousarray(w.reshape(KC, 128, M // N, N).transpose(2, 1, 0, 3))


def _col(v):
    return np.ascontiguousarray(np.asarray(v, np.float32).reshape(-1, 128).T)


def prep_shared(inp):
    f = lambda k: np.asarray(inp[k], np.float32)
    w_in = f("w_in")
    sh = {}
    sh["wmod"] = np.ascontiguousarray(f("w_mod").reshape(L, KD, 128, 6 * D))
    vecs = np.zeros((L, 128, NV), np.float32)
    rv = np.zeros((L, NR), np.float32)
    wa, wb, wb2 = [], [], []
    for l in range(L):
        vecs[l, :, V_BMOD:V_BMOD + 96] = _col(f("b_mod")[l])
        vecs[l, :, V_N1:V_N1 + 16] = _col(f("norm1_w")[l])
        vecs[l, :, V_N2:V_N2 + 16] = _col(f("norm2_w")[l])
        for k in range(4):
            vecs[l, :, V_CW + k * 48:V_CW + (k + 1) * 48] = _col(f("ssm_conv_w")[l, k])
        vecs[l, :, V_CB:V_CB + 48] = _col(f("ssm_conv_b")[l])
        vecs[l, :, V_SNW:V_SNW + 32] = _col(f("ssm_norm_w")[l])
        for k in range(3):
            vecs[l, :, V_FW + k * 44:V_FW + (k + 1) * 44] = _col(f("ffn_conv_w")[l, k])
        vecs[l, :, V_FB:V_FB + 44] = _col(f("ffn_conv_b")[l])
        rv[l, R_DTB:R_DTB + 64] = f("ssm_dt_bias")[l]
        rv[l, R_ALOG:R_ALOG + 64] = f("ssm_a_log")[l]
        rv[l, R_D:R_D + 64] = f("ssm_d")[l]
        rv[l, R_QNW:R_QNW + 512] = f("mla_q_norm_w")[l]
        rv[l, R_KVNW:R_KVNW + 512] = f("mla_kv_norm_w")[l]
        rv[l, R_QKQ:R_QKQ + 192] = f("qk_norm_q_w")[l]
        rv[l, R_QKK:R_QKK + 192] = f("qk_norm_k_w")[l]
        w = w_in[l]
        z, xbc, dt_, ql, kvl, kr, ga, gb = (w[:, 0:4096], w[:, 4096:10240], w[:, 10240:10304], w[:, 10304:10816],
                                            w[:, 10816:11328], w[:, 11328:11392], w[:, 11392:13440], w[:, 13440:15488])
        wa.append(_tile_fm(np.concatenate([xbc, ga, gb], axis=1), KD))
        wb.append(_tile_tm(np.concatenate([z, ql, kvl], axis=1), KD, 512))
        wb2.append(_tile_tm(np.concatenate([dt_, kr], axis=1), KD, 128)[0])
    sh["vecs"] = vecs
    sh["rv"] = rv
    sh["wa"] = np.stack(wa)
    sh["wb"] = np.stack(wb)
    sh["wb2"] = np.stack(wb2)
    sh["wqup"] = np.ascontiguousarray(f("w_q_up").reshape(L, 4, 128, 3072).transpose(0, 2, 1, 3))
    sh["wkvup"] = np.ascontiguousarray(f("w_kv_up").reshape(L, 4, 128, 4096).transpose(0, 2, 1, 3))
    sh["wsso"] = np.stack([_tile_fm(f("w_ssm_out")[l], 32) for l in range(L)])
    sh["wmo"] = np.stack([_tile_fm(f("w_mla_out")[l], 16) for l in range(L)])
    sh["wmix"] = np.stack([_tile_fm(f("w_mix_out")[l], 16) for l in range(L)])
    sh["wup"] = np.stack([_tile_fm(f("w_ffn_up")[l], KD) for l in range(L)])
    sh["wdn"] = np.stack([_tile_fm(f("w_ffn_down")[l], KF) for l in range(L)])
    kk = np.arange(128)
    sh["ident"] = np.eye(128, dtype=np.float32)
    sh["triu"] = (kk[:, None] <= kk[None, :]).astype(np.float32)
    sh["mneg"] = np.where(kk[:, None] <= kk[None, :], 0.0, -30000.0).astype(np.float32)
    sh["dmask"] = ((kk[:, None] < 64) | (kk[None, :] >= 64)).astype(np.float32)
    sh["invf"] = (np.float32(10000.0) ** (-np.arange(0, 64, 2, dtype=np.float32) / np.float32(64))).astype(np.float32)
    return sh


def prep_core(inp, b, T):
    NT = T // 128
    return {
        "xT": np.ascontiguousarray(np.asarray(inp["x"][b, :T], np.float32).T),
        "cT": _col(np.asarray(inp["c"][b], np.float32)),
        "pos": np.ascontiguousarray(np.asarray(inp["positions"][b, :T], np.int32).reshape(NT, 128).T),
    }


_NC_CACHE = {}


def run(inp, T, batches, dbg=(), stop=None):
    key = (T, tuple(dbg), stop)
    if key not in _NC_CACHE:
        _NC_CACHE[key] = build(T, dbg, stop)
    nc = _NC_CACHE[key]
    sh = prep_shared(inp)
    in_maps = []
    for b in batches:
        m = dict(sh)
        m.update(prep_core(inp, b, T))
        in_maps.append(m)
    res = run_bass_kernel_spmd(nc, in_maps, core_ids=list(range(len(batches))))
    return res.results


def kernel(**inputs):
    B, T = inputs["x"].shape[0], inputs["x"].shape[1]
    results = run(inputs, T, list(range(B)))
    out = np.stack([np.ascontiguousarray(r["outT"].T) for r in results], axis=0)
    return out.astype(np.float32)
```

```python
import numpy as np
from contextlib import ExitStack
import concourse.bass as bass
import concourse.mybir as mybir
from concourse.bass_utils import run_bass_kernel_spmd

F32 = mybir.dt.float32
BF16 = mybir.dt.bfloat16
I32 = mybir.dt.int32
AF = mybir.ActivationFunctionType
ALU = mybir.AluOpType
AX = mybir.AxisListType

SAME_ENGINE_SYNC = ('pool', 'act')

D = 2048
KD = 16
DI = 4096
NH = 64
NG = 8
FF = 5632
KF = 44
L = 2
EPS = 1e-6
PI = float(np.pi)


class Buf:
    __slots__ = ("t", "name", "w", "r", "sem", "psum")

    def __init__(self, t, name, psum=False):
        self.psum = psum
        self.t = t
        self.name = name
        self.w = None
        self.r = {}
        self.sem = None

    def __getitem__(self, k):
        return self.t[k]


class EngS:
    def __init__(self, name, eng, sem):
        self.name = name
        self.eng = eng
        self.sem = sem
        self.count = 0
        self.waited = {}


class Ring:
    def __init__(self, bufs):
        self.bufs = bufs
        self.i = 0

    def next(self):
        b = self.bufs[self.i % len(self.bufs)]
        self.i += 1
        return b

    def prev(self):
        return self.bufs[(self.i - 2) % len(self.bufs)]


class KB:
    def __init__(self, nc, n_dma_sems=80):
        self.nc = nc
        self.root = ExitStack()
        self.E = {}
        for name, eng in (("pe", nc.tensor), ("act", nc.scalar), ("dve", nc.vector),
                          ("pool", nc.gpsimd), ("sp", nc.sync)):
            sem = self.root.enter_context(nc.semaphore("sem_" + name))
            self.E[name] = EngS(name, eng, sem)
        self.dsems = []
        for i in range(n_dma_sems):
            self.dsems.append([self.root.enter_context(nc.semaphore("dsem%d" % i)), 0])
        self.free_dsems = list(range(n_dma_sems))
        self.stacks = [self.root]
        self.phase_bufs = [[]]
        self.uid = 0
        self.dummy = Buf(None, "dummy")
        self.phase_bufs[0].append(self.dummy)

    def push(self):
        self.stacks.append(ExitStack())
        self.phase_bufs.append([])

    def pop(self):
        self.barrier()
        for b in self.phase_bufs.pop():
            if b.sem is not None:
                self.free_dsems.append(b.sem)
                b.sem = None
        self.stacks.pop().close()

    def tile(self, name, shape, dtype, space="sbuf"):
        es = self.stacks[-1]
        self.uid += 1
        uname = "%s_%d" % (name, self.uid)
        if space == "sbuf":
            t = es.enter_context(self.nc.sbuf_tensor(uname, list(shape), dtype))
        else:
            t = es.enter_context(self.nc.psum_tensor(uname, list(shape), dtype))
        b = Buf(t, uname, space != "sbuf")
        self.phase_bufs[-1].append(b)
        return b

    def pool(self, name, shape, dtype, n, space="sbuf"):
        return Ring([self.tile("%s%d" % (name, i), shape, dtype, space) for i in range(n)])

    def _wait(self, E, ev):
        if ev is None:
            return
        if ev[0] == "e":
            _, en, cnt, tiny = ev
            if en == E.name and en not in SAME_ENGINE_SYNC and not (tiny and en != "pe"):
                return
            if E.waited.get(en, 0) >= cnt:
                return
            E.eng.wait_ge(self.E[en].sem, cnt)
            E.waited[en] = cnt
        else:
            slot = ev[1]
            h, tot = self.dsems[slot]
            key = ("d", slot)
            if E.waited.get(key, 0) >= tot:
                return
            E.eng.wait_ge(h, tot)
            E.waited[key] = tot

    def _deps(self, E, reads, writes):
        for b in reads:
            self._wait(E, b.w)
            if b.psum:
                for en2, ev in list(b.r.items()):
                    if en2 != E.name:
                        self._wait(E, ev)
        for b in writes:
            self._wait(E, b.w)
            for ev in list(b.r.values()):
                self._wait(E, ev)

    def op(self, en, fn, reads=(), writes=(), tiny=False):
        E = self.E[en]
        self._deps(E, reads, writes)
        inst = fn(E.eng)
        E.count += 1
        inst.then_inc(E.sem, 1)
        ev = ("e", en, E.count, tiny)
        for b in reads:
            b.r[en] = ev
        for b in writes:
            b.w = ev
            b.r = {}
        return inst

    def dma(self, qn, out, in_, sb, load, first=True, **kw):
        E = self.E[qn]
        if first:
            if load:
                self._deps(E, (), [sb])
            else:
                self._deps(E, [sb], ())
        if sb.sem is None:
            sb.sem = self.free_dsems.pop()
        slot = self.dsems[sb.sem]
        inst = E.eng.dma_start(out=out, in_=in_, **kw)
        inst.then_inc(slot[0], 16)
        slot[1] += 16
        ev = ("d", sb.sem)
        if load:
            sb.w = ev
            sb.r = {}
        else:
            sb.r["dma"] = ev
        return inst

    def barrier(self):
        used = [i for i in range(len(self.dsems)) if self.dsems[i][1] > 0]
        for E in self.E.values():
            for E2 in self.E.values():
                if E2 is E or E2.count == 0:
                    continue
                if E.waited.get(E2.name, 0) < E2.count:
                    E.eng.wait_ge(E2.sem, E2.count)
                    E.waited[E2.name] = E2.count
            for i in used:
                h, tot = self.dsems[i]
                key = ("d", i)
                if E.waited.get(key, 0) < tot:
                    E.eng.wait_ge(h, tot)
                    E.waited[key] = tot
        for bl in self.phase_bufs:
            for b in bl:
                b.w = None
                b.r = {}


V_BMOD = 0
V_N1 = 96
V_N2 = 112
V_CW = 128
V_CB = 320
V_SNW = 368
V_FW = 400
V_FB = 532
NV = 576
R_DTB = 0
R_ALOG = 64
R_D = 128
R_QNW = 192
R_KVNW = 704
R_QKQ = 1216
R_QKK = 1408
NR = 1600


class _Stop(Exception):
    pass


def build(T, dbg=(), stop=None):
    NT = T // 128
    NTB = T // 512
    nc = bass.Bass("TRN2", target_bir_lowering=False)

    def din(name, shape, dt=F32):
        return nc.dram_tensor(name, list(shape), dt, kind="ExternalInput").ap()

    def dscr(name, shape, dt):
        kind = "ExternalOutput" if name in dbg else "Internal"
        return nc.dram_tensor(name, list(shape), dt, kind=kind).ap()

    xT_in = din("xT", [D, T])
    cT_in = din("cT", [128, KD])
    pos_in = din("pos", [128, NT], I32)
    invf_in = din("invf", [32])
    ident_in = din("ident", [128, 128])
    triu_in = din("triu", [128, 128])
    mneg_in = din("mneg", [128, 128])
    dmask_in = din("dmask", [128, 128])
    wmod_in = din("wmod", [L, KD, 128, 6 * D])
    vecs_in = din("vecs", [L, 128, NV])
    rv_in = din("rv", [L, NR])
    wa_in = din("wa", [L, 80, 128, KD, 128])
    wb_in = din("wb", [L, 10, 128, KD, 512])
    wb2_in = din("wb2", [L, 128, KD, 128])
    wqup_in = din("wqup", [L, 128, 4, 3072])
    wkvup_in = din("wkvup", [L, 128, 4, 4096])
    wsso_in = din("wsso", [L, 16, 128, 32, 128])
    wmo_in = din("wmo", [L, 16, 128, 16, 128])
    wmix_in = din("wmix", [L, 16, 128, 16, 128])
    wup_in = din("wup", [L, 88, 128, KD, 128])
    wdn_in = din("wdn", [L, 16, 128, KF, 128])
    xT = nc.dram_tensor("outT", [D, T], F32, kind="ExternalOutput").ap()

    hT = dscr("hT", [D, T], BF16)
    zs_tm = dscr("zs_tm", [T, DI], BF16)
    lat_tm = dscr("lat_tm", [T, 1088], BF16)
    dt_tm = dscr("dt_tm", [T, 64], F32)
    x_tm = dscr("x_tm", [T, DI], BF16)
    B_tm = dscr("B_tm", [T, 1024], BF16)
    BT = dscr("BT", [1024, T], BF16)
    CT = dscr("CT", [1024, T], BF16)
    sga = dscr("sga", [D, T], BF16)
    sgb = dscr("sgb", [D, T], BF16)
    acsB = dscr("acsB", [NT, 64 * 128], F32)
    acs2 = dscr("acs2", [T, 128], F32)
    qTn = dscr("qTn", [16, 128, T], BF16)
    qTr = dscr("qTr", [16, 64, T], BF16)
    kTn = dscr("kTn", [16, 128, T], BF16)
    kTr = dscr("kTr", [16, 64, T], BF16)
    V_tm = dscr("V_tm", [16, 128, T], BF16)
    oT = dscr("oT", [D, T], BF16)
    gT = dscr("gT", [DI, T], BF16)
    mergedT = dscr("mergedT", [D, T], BF16)
    actT = dscr("actT", [FF, T], BF16)

    kb = KB(nc)
    op = kb.op

    def chk(k):
        if stop == k:
            kb.barrier()
            raise _Stop()

    try:
     with kb.root:
        identf = kb.tile("identf", [128, 128], F32)
        identb = kb.tile("identb", [128, 128], BF16)
        onesb = kb.tile("onesb", [128, 128], BF16)
        triu = kb.tile("triu", [128, 128], F32)
        mneg = kb.tile("mneg", [128, 128], F32)
        dmask = kb.tile("dmask", [128, 128], BF16)
        cs = kb.tile("cs", [128, NT, 64], F32)
        condT = kb.tile("condT", [128, KD, 2], F32)
        epsc = kb.tile("epsc", [128, 1], F32)

        kb.push()
        tmpf = kb.tile("tmpf", [128, 128], F32)
        kb.dma("sp", identf[:], ident_in[:, :], identf, True)
        kb.dma("sp", triu[:], triu_in[:, :], triu, True)
        kb.dma("sp", mneg[:], mneg_in[:, :], mneg, True)
        kb.dma("sp", tmpf[:], dmask_in[:, :], tmpf, True)
        op("dve", lambda e: e.tensor_copy(out=dmask[:], in_=tmpf[:]), [tmpf], [dmask], tiny=True)
        op("dve", lambda e: e.tensor_copy(out=identb[:], in_=identf[:]), [identf], [identb], tiny=True)
        op("pool", lambda e: e.memset(onesb[:], 1.0), (), [onesb])
        op("pool", lambda e: e.memset(epsc[:], EPS), (), [epsc])
        ct = kb.tile("ct", [128, KD], F32)
        kb.dma("sp", ct[:], cT_in[:, :], ct, True)
        op("act", lambda e: e.activation(out=condT[:, :, 0], in_=ct[:], func=AF.Silu), [ct], [condT])
        op("act", lambda e: e.activation(out=condT[:, :, 1], in_=ct[:], func=AF.Silu), [ct], [condT])
        posi = kb.tile("posi", [128, NT], I32)
        posf = kb.tile("posf", [128, NT], F32)
        invf = kb.tile("invf", [128, 32], F32)
        ang = kb.tile("ang", [128, NT, 32], F32)
        a2 = kb.tile("a2", [128, NT, 32], F32)
        uu = kb.tile("uu", [128, NT, 32], F32)
        ki = kb.tile("ki", [128, NT, 32], I32)
        kf = kb.tile("kf", [128, NT, 32], F32)
        kb.dma("sp", posi[:], pos_in[:, :], posi, True)
        kb.dma("sp", invf[:], invf_in.partition_broadcast(128), invf, True)
        op("dve", lambda e: e.tensor_copy(out=posf[:], in_=posi[:]), [posi], [posf], tiny=True)
        op("dve", lambda e: e.tensor_tensor(out=ang[:], in0=posf[:].unsqueeze(2).to_broadcast([128, NT, 32]),
                                            in1=invf[:].unsqueeze(1).to_broadcast([128, NT, 32]), op=ALU.mult),
           [posf, invf], [ang], tiny=True)
        C1 = 6.28125
        C2 = float(2 * np.pi - 6.28125)
        for off, lo in ((PI / 2, 0), (0.0, 32)):
            op("dve", lambda e, off=off: e.tensor_scalar(out=a2[:], in0=ang[:], scalar1=off, scalar2=None, op0=ALU.add), [ang], [a2], tiny=True)
            op("dve", lambda e: e.tensor_scalar(out=uu[:], in0=a2[:], scalar1=float(1 / (2 * np.pi)), scalar2=None, op0=ALU.mult), [a2], [uu], tiny=True)
            op("dve", lambda e: e.tensor_copy(out=ki[:], in_=uu[:]), [uu], [ki], tiny=True)
            op("dve", lambda e: e.tensor_copy(out=kf[:], in_=ki[:]), [ki], [kf], tiny=True)
            op("dve", lambda e: e.scalar_tensor_tensor(out=a2[:], in0=kf[:], scalar=-C1, in1=a2[:], op0=ALU.mult, op1=ALU.add), [kf, a2], [a2], tiny=True)
            op("dve", lambda e: e.scalar_tensor_tensor(out=a2[:], in0=kf[:], scalar=-C2, in1=a2[:], op0=ALU.mult, op1=ALU.add), [kf, a2], [a2], tiny=True)
            op("dve", lambda e: e.tensor_scalar(out=uu[:], in0=a2[:], scalar1=PI, scalar2=-2 * PI, op0=ALU.is_gt, op1=ALU.mult), [a2], [uu], tiny=True)
            op("dve", lambda e: e.tensor_tensor(out=a2[:], in0=a2[:], in1=uu[:], op=ALU.add), [a2, uu], [a2], tiny=True)
            op("dve", lambda e: e.tensor_scalar(out=uu[:], in0=a2[:], scalar1=-PI, scalar2=2 * PI, op0=ALU.is_lt, op1=ALU.mult), [a2], [uu], tiny=True)
            op("dve", lambda e: e.tensor_tensor(out=a2[:], in0=a2[:], in1=uu[:], op=ALU.add), [a2, uu], [a2], tiny=True)
            op("dve", lambda e: e.tensor_scalar(out=a2[:], in0=a2[:], scalar1=-PI, scalar2=PI, op0=ALU.max, op1=ALU.min), [a2], [a2], tiny=True)
            op("act", lambda e, lo=lo: e.activation(out=cs[:, :, lo:lo + 32], in_=a2[:], func=AF.Sin), [a2], [cs])
        kb.dma("sp", xT[:, :], xT_in[:, :], kb.dummy, True)
        kb.pop()

        chk('setup')
        def gemm_fm(TBLK, acts, jobs, epi, pre=None):
            TBLK = min(TBLK, T)
            nstream = max(len(j) for j in jobs)
            abufs = [kb.tile("act%d" % i, [128, KC, TBLK], BF16) for i, (_, KC) in enumerate(acts)]
            wkc = [0] * nstream
            for j in jobs:
                for si, (ai, wd, M) in enumerate(j):
                    wkc[si] = max(wkc[si], acts[ai][1])
            wpools = [kb.pool("w%d" % si, [128, wkc[si], 128], BF16, 3) for si in range(nstream)]
            pspools = [kb.pool("ps%d" % si, [128, 512], F32, 3, "psum") for si in range(nstream)]

            def loadw(job):
                res = []
                for si, (ai, wd, M) in enumerate(job):
                    KC = acts[ai][1]
                    w = wpools[si].next()
                    for k0 in range(0, KC, 16):
                        k1 = min(KC, k0 + 16)
                        kb.dma("pool", w[:, k0:k1, :M], wd[:, k0:k1, :], w, True, first=(k0 == 0))
                    res.append(w)
                return res

            pending = None
            for blk in range(T // TBLK):
                for ai, (ad, KC) in enumerate(acts):
                    src = ad.rearrange("(kc p) t -> p kc t", p=128)
                    for k0 in range(0, KC, 4):
                        kb.dma("sp", abufs[ai][:, k0:k0 + 4, :], src[:, k0:k0 + 4, blk * TBLK:(blk + 1) * TBLK],
                               abufs[ai], True, first=(k0 == 0))
                nxt = loadw(jobs[0])
                for ji, job in enumerate(jobs):
                    cur = nxt
                    if ji + 1 < len(jobs):
                        nxt = loadw(jobs[ji + 1])
                    for sb in range(TBLK // 512):
                        pss = []
                        for si, (ai, wd, M) in enumerate(job):
                            KC = acts[ai][1]
                            ps = pspools[si].next()
                            w = cur[si]
                            a = abufs[ai]
                            for kc in range(KC):
                                op("pe", lambda e, ps=ps, w=w, a=a, kc=kc, M=M, KC=KC, sb=sb: e.matmul(
                                    ps[:M, :], lhsT=w[:, kc, :M], rhs=a[:, kc, sb * 512:(sb + 1) * 512],
                                    start=(kc == 0), stop=(kc == KC - 1)), [w, a], [ps])
                            pss.append(ps)
                        pctx = pre(ji, blk * TBLK + sb * 512) if pre is not None else None
                        if pending is not None:
                            epi(*pending)
                        pending = (ji, blk * TBLK + sb * 512, pss, pctx)
            if pending is not None:
                epi(*pending)

        def gemm_tm(ad, blocks, epi):
            a = kb.tile("acttm", [128, KD, T], BF16)
            src = ad.rearrange("(kc p) t -> p kc t", p=128)
            for k0 in range(0, KD, 4):
                kb.dma("sp", a[:, k0:k0 + 4, :], src[:, k0:k0 + 4, :], a, True, first=(k0 == 0))
            wpool = kb.pool("wtm", [128, KD, 512], BF16, 2)
            pspool = kb.pool("pstm", [128, 512], F32, 3, "psum")

            def loadw(bi):
                wd, N = blocks[bi]
                w = wpool.next()
                for k0 in range(0, KD, 4):
                    kb.dma("pool", w[:, k0:k0 + 4, :N], wd[:, k0:k0 + 4, :], w, True, first=(k0 == 0))
                return w
            nxt = loadw(0)
            for bi, (wd, N) in enumerate(blocks):
                w = nxt
                if bi + 1 < len(blocks):
                    nxt = loadw(bi + 1)
                for tt in range(NT):
                    ps = pspool.next()
                    for kc in range(KD):
                        op("pe", lambda e, ps=ps, w=w, kc=kc, N=N, tt=tt: e.matmul(
                            ps[:, :N], lhsT=a[:, kc, tt * 128:(tt + 1) * 128], rhs=w[:, kc, :N],
                            start=(kc == 0), stop=(kc == KD - 1)), [w, a], [ps])
                    epi(bi, tt, ps)

        def norm_phase(Acol, Bcol, mvb):
            kb.push()
            xbp = kb.pool("xb", [128, KD, 512], F32, 2)
            sq = kb.tile("sq", [128, KD, 512], BF16)
            hbp = kb.pool("hb", [128, KD, 512], BF16, 2)
            tmp = kb.tile("tmp", [128, KD, 512], F32)
            rsp = kb.pool("rstd", [128, 512], F32, 2)
            psp = kb.pool("psn", [128, 512], F32, 2, "psum")
            src = xT.rearrange("(kc p) t -> p kc t", p=128)
            dst = hT.rearrange("(kc p) t -> p kc t", p=128)
            for blk in range(NTB):
                sl = slice(blk * 512, (blk + 1) * 512)
                xb = xbp.next()
                for k0 in range(0, KD, 4):
                    kb.dma("sp", xb[:, k0:k0 + 4, :], src[:, k0:k0 + 4, sl], xb, True, first=(k0 == 0))
                op("act", lambda e, xb=xb: e.activation(out=sq[:], in_=xb[:], func=AF.Square), [xb], [sq])
                ps = psp.next()
                for kc in range(KD):
                    op("pe", lambda e, ps=ps, kc=kc: e.matmul(ps[:], lhsT=onesb[:], rhs=sq[:, kc, :],
                                                              start=(kc == 0), stop=(kc == KD - 1)), [onesb, sq], [ps])
                rs = rsp.next()
                op("act", lambda e, rs=rs, ps=ps: e.activation(out=rs[:], in_=ps[:], func=AF.Ln, bias=epsc[:], scale=1.0 / D), [ps, epsc], [rs])
                op("act", lambda e, rs=rs: e.activation(out=rs[:], in_=rs[:], func=AF.Exp, scale=-0.5), [rs], [rs])
                op("dve", lambda e, xb=xb, rs=rs: e.tensor_tensor(out=tmp[:], in0=xb[:], in1=rs[:].unsqueeze(1).to_broadcast([128, KD, 512]), op=ALU.mult), [xb, rs], [tmp])
                hb = hbp.next()
                for kc in range(KD):
                    en = "act" if kc % 2 == 0 else "dve"
                    if en == "act":
                        op("act", lambda e, hb=hb, kc=kc: e.activation(out=hb[:, kc, :], in_=tmp[:, kc, :], func=AF.Identity,
                                                                       scale=Acol(kc), bias=Bcol(kc)), [tmp, mvb], [hb])
                    else:
                        op("dve", lambda e, hb=hb, kc=kc: e.tensor_scalar(out=hb[:, kc, :], in0=tmp[:, kc, :], scalar1=Acol(kc),
                                                                         scalar2=Bcol(kc), op0=ALU.mult, op1=ALU.add), [tmp, mvb], [hb])
                for k0 in range(0, KD, 4):
                    kb.dma("sp", dst[:, k0:k0 + 4, sl], hb[:, k0:k0 + 4, :], hb, False, first=(k0 == 0))
            kb.pop()

        for l in range(L):
            kb.push()
            vecs = kb.tile("vecs", [128, NV], F32)
            rv = kb.tile("rv", [128, NR], F32)
            mv = kb.tile("mv", [128, 6, KD], F32)
            aneg = kb.tile("aneg", [128, 64], F32)
            kb.dma("sp", vecs[:], vecs_in[l], vecs, True)
            kb.dma("sp", rv[:], rv_in[l].partition_broadcast(128), rv, True)
            op("act", lambda e: e.activation(out=aneg[:], in_=rv[:, R_ALOG:R_ALOG + 64], func=AF.Exp), [rv], [aneg], tiny=True)
            op("dve", lambda e: e.tensor_scalar(out=aneg[:], in0=aneg[:], scalar1=-1.0, scalar2=None, op0=ALU.mult), [aneg], [aneg], tiny=True)

            kb.push()
            wmp = kb.pool("wm", [128, 6 * D], F32, 2)
            acc = kb.tile("macc", [128, 192], F32)
            modv = kb.tile("modv", [128, 96], F32)
            psm = kb.pool("psm", [128, 512], F32, 2, "psum")
            for kc in range(KD):
                w = wmp.next()
                for q in range(4):
                    kb.dma("sp", w[:, q * 3072:(q + 1) * 3072], wmod_in[l, kc][:, q * 3072:(q + 1) * 3072], w, True, first=(q == 0))
                ps = psm.next()
                for m in range(96):
                    op("pe", lambda e, ps=ps, w=w, m=m, kc=kc: e.matmul(ps[:, 2 * m:2 * m + 2], lhsT=w[:, m * 128:(m + 1) * 128],
                                                                      rhs=condT[:, kc, :], start=True, stop=True), [w, condT], [ps])
                if kc == 0:
                    op("dve", lambda e, ps=ps: e.tensor_copy(out=acc[:], in_=ps[:, 0:192]), [ps], [acc], tiny=True)
                else:
                    op("dve", lambda e, ps=ps: e.tensor_tensor(out=acc[:], in0=acc[:], in1=ps[:, 0:192], op=ALU.add), [ps, acc], [acc], tiny=True)
            op("dve", lambda e: e.tensor_tensor(out=modv[:], in0=acc[:].rearrange("p (m two) -> p m two", two=2)[:, :, 0],
                                                in1=vecs[:, V_BMOD:V_BMOD + 96], op=ALU.add), [acc, vecs], [modv], tiny=True)
            op("dve", lambda e: e.scalar_tensor_tensor(out=mv[:, 0, :], in0=modv[:, 16:32], scalar=1.0, in1=vecs[:, V_N1:V_N1 + 16], op0=ALU.add, op1=ALU.mult), [modv, vecs], [mv], tiny=True)
            op("dve", lambda e: e.tensor_copy(out=mv[:, 1, :], in_=modv[:, 0:16]), [modv], [mv], tiny=True)
            op("dve", lambda e: e.tensor_copy(out=mv[:, 2, :], in_=modv[:, 32:48]), [modv], [mv], tiny=True)
            op("dve", lambda e: e.scalar_tensor_tensor(out=mv[:, 3, :], in0=modv[:, 64:80], scalar=1.0, in1=vecs[:, V_N2:V_N2 + 16], op0=ALU.add, op1=ALU.mult), [modv, vecs], [mv], tiny=True)
            op("dve", lambda e: e.tensor_copy(out=mv[:, 4, :], in_=modv[:, 48:64]), [modv], [mv], tiny=True)
            op("dve", lambda e: e.tensor_copy(out=mv[:, 5, :], in_=modv[:, 80:96]), [modv], [mv], tiny=True)
            kb.pop()

            chk('p1')
            norm_phase(lambda kc: mv[:, 0, kc:kc + 1], lambda kc: mv[:, 1, kc:kc + 1], mv)

            chk('p2')
            kb.push()
            cbp = kb.pool("cb", [128, 515], F32, 2)
            accp = kb.pool("cacc", [128, 512], F32, 2)
            up = kb.pool("u", [128, 512], BF16, 3)
            utp = kb.pool("utm", [128, 4, 128], BF16, 2)
            pstp = kb.pool("pst", [128, 8, 128], BF16, 1, "psum")

            def epi_a(ji, t0, pss, pctx=None):
                ps = pss[0]
                sl = slice(t0, t0 + 512)
                if ji < 48:
                    cb = cbp.next()
                    if t0 == 0:
                        op("pool", lambda e: e.memset(cb[:, 0:3], 0.0), (), [cb])
                    else:
                        pv = cbp.prev()
                        op("pool", lambda e: e.tensor_copy(out=cb[:, 0:3], in_=pv[:, 512:515]), [pv], [cb])
                    op("act", lambda e: e.activation(out=cb[:, 3:515], in_=ps[:], func=AF.Copy), [ps], [cb])
                    ac = accp.next()
                    cw = lambda k: vecs[:, V_CW + k * 48 + ji:V_CW + k * 48 + ji + 1]
                    op("dve", lambda e: e.tensor_scalar(out=ac[:], in0=cb[:, 0:512], scalar1=cw(0), scalar2=vecs[:, V_CB + ji:V_CB + ji + 1],
                                                        op0=ALU.mult, op1=ALU.add), [cb, vecs], [ac])
                    for k in (1, 2, 3):
                        op("dve", lambda e, k=k: e.scalar_tensor_tensor(out=ac[:], in0=cb[:, k:k + 512], scalar=cw(k), in1=ac[:],
                                                                        op0=ALU.mult, op1=ALU.add), [cb, vecs, ac], [ac])
                    u = up.next()
                    op("act", lambda e: e.activation(out=u[:], in_=ac[:], func=AF.Silu), [ac], [u])
                    if ji >= 32:
                        dstT = BT if ji < 40 else CT
                        r0 = (ji - 32) * 128 if ji < 40 else (ji - 40) * 128
                        kb.dma("sp", dstT[r0:r0 + 128, sl], u[:], u, False)
                    if ji < 40:
                        pst = pstp.next()
                        for j in range(4):
                            op("pe", lambda e, j=j: e.transpose(out=pst[:, j, :], in_=u[:, j * 128:(j + 1) * 128], identity=identb[:]), [u, identb], [pst])
                        ut = utp.next()
                        op("dve", lambda e: e.tensor_copy(out=ut[:], in_=pst[:, 0:4, :]), [pst], [ut])
                        if ji < 32:
                            dd = x_tm[t0:t0 + 512, ji * 128:(ji + 1) * 128]
                        else:
                            dd = B_tm[t0:t0 + 512, (ji - 32) * 128:(ji - 31) * 128]
                        kb.dma("sp", dd.rearrange("(j p) c -> p j c", p=128), ut[:], ut, False)
                else:
                    u = up.next()
                    op("act", lambda e: e.activation(out=u[:], in_=ps[:], func=AF.Sigmoid), [ps], [u])
                    dstT = sga if ji < 64 else sgb
                    r0 = (ji - 48) * 128 if ji < 64 else (ji - 64) * 128
                    kb.dma("sp", dstT[r0:r0 + 128, sl], u[:], u, False)

            gemm_fm(T, [(hT, KD)], [[(0, wa_in[l, i], 128)] for i in range(80)], epi_a)
            kb.pop()

            chk('p3a')
            kb.push()
            ztp = kb.pool("zt", [128, 512], BF16, 3)
            dxp = kb.pool("dtx", [128, 64], F32, 2)
            dtp = kb.pool("dtt", [128, 64], F32, 2)
            dop = kb.pool("dto", [128, 64], F32, 2)

            def epi_b(bi, tt, ps):
                rows = slice(tt * 128, (tt + 1) * 128)
                zt = ztp.next()
                if bi < 8:
                    op("act", lambda e: e.activation(out=zt[:], in_=ps[:], func=AF.Silu), [ps], [zt])
                    kb.dma("sp", zs_tm[rows, bi * 512:(bi + 1) * 512], zt[:], zt, False)
                elif bi < 10:
                    op("act", lambda e: e.activation(out=zt[:], in_=ps[:], func=AF.Copy), [ps], [zt])
                    kb.dma("sp", lat_tm[rows, (bi - 8) * 512:(bi - 7) * 512], zt[:], zt, False)
                else:
                    op("act", lambda e: e.activation(out=zt[:, 0:64], in_=ps[:, 64:128], func=AF.Copy), [ps], [zt])
                    kb.dma("sp", lat_tm[rows, 1024:1088], zt[:, 0:64], zt, False)
                    dx = dxp.next()
                    dt_ = dtp.next()
                    do = dop.next()
                    op("dve", lambda e: e.tensor_tensor(out=dx[:], in0=ps[:, 0:64], in1=rv[:, R_DTB:R_DTB + 64], op=ALU.add), [ps, rv], [dx], tiny=True)
                    op("dve", lambda e: e.tensor_scalar(out=dt_[:], in0=dx[:], scalar1=30.0, scalar2=None, op0=ALU.min), [dx], [dt_], tiny=True)
                    op("act", lambda e: e.activation(out=dt_[:], in_=dt_[:], func=AF.Exp), [dt_], [dt_], tiny=True)
                    op("act", lambda e: e.activation(out=dt_[:], in_=dt_[:], func=AF.Ln, bias=1.0), [dt_], [dt_], tiny=True)
                    op("dve", lambda e: e.tensor_tensor(out=do[:], in0=dx[:], in1=dt_[:], op=ALU.max), [dx, dt_], [do], tiny=True)
                    kb.dma("sp", dt_tm[rows, :], do[:], do, False)

            blocks = [(wb_in[l, i], 512) for i in range(10)] + [(wb2_in[l], 128)]
            gemm_tm(hT, blocks, epi_b)
            kb.pop()

            chk('p3b')
            kb.push()
            dlp = kb.pool("dl", [128, 64], F32, 2)
            dap = kb.pool("da", [128, 64], F32, 2)
            lnp = kb.pool("ln", [128, 64], F32, 2)
            a2p = kb.pool("a2", [128, 128], F32, 2)
            atp = kb.pool("at", [64, 128], F32, 2)
            psa = kb.pool("psa", [128, 512], F32, 2, "psum")
            pstf = kb.pool("pstf", [128, 512], F32, 2, "psum")
            for c in range(NT):
                rows = slice(c * 128, (c + 1) * 128)
                dl = dlp.next()
                kb.dma("sp", dl[:], dt_tm[rows, :], dl, True)
                da = dap.next()
                op("dve", lambda e: e.tensor_tensor(out=da[:], in0=dl[:], in1=aneg[:], op=ALU.mult), [dl, aneg], [da], tiny=True)
                ps = psa.next()
                op("pe", lambda e: e.matmul(ps[:, 0:64], lhsT=triu[:], rhs=da[:], start=True, stop=True), [triu, da], [ps])
                ln_ = lnp.next()
                op("act", lambda e: e.activation(out=ln_[:], in_=dl[:], func=AF.Ln), [dl], [ln_])
                a2_ = a2p.next()
                op("dve", lambda e: e.tensor_copy(out=a2_[:, 64:128], in_=ps[:, 0:64]), [ps], [a2_], tiny=True)
                op("dve", lambda e: e.tensor_tensor(out=a2_[:, 0:64], in0=a2_[:, 64:128], in1=ln_[:], op=ALU.subtract), [a2_, ln_], [a2_], tiny=True)
                kb.dma("sp", acs2[rows, :], a2_[:], a2_, False)
                pt = pstf.next()
                op("pe", lambda e: e.transpose(out=pt[:64, 0:128], in_=a2_[:, 64:128], identity=identf[:]), [a2_, identf], [pt])
                at = atp.next()
                op("dve", lambda e: e.tensor_copy(out=at[:], in_=pt[:64, 0:128]), [pt], [at], tiny=True)
                kb.dma("sp", acsB[c].rearrange("(r q) -> r q", q=128), at[:], at, False)
            kb.pop()

            chk('p3c')
            kb.push()
            wq = kb.tile("wq", [128, 4, 3072], BF16)
            wkv = kb.tile("wkv", [128, 4, 4096], BF16)
            for kc in range(4):
                for c0 in range(0, 3072, 1024):
                    kb.dma("pool", wq[:, kc, c0:c0 + 1024], wqup_in[l][:, kc, c0:c0 + 1024], wq, True, first=(kc == 0 and c0 == 0))
                for c0 in range(0, 4096, 1024):
                    kb.dma("pool", wkv[:, kc, c0:c0 + 1024], wkvup_in[l][:, kc, c0:c0 + 1024], wkv, True, first=(kc == 0 and c0 == 0))
            latp = kb.pool("lat", [128, 1088], BF16, 2)
            junk = kb.tile("junk", [128, 3072], F32)
            junka = kb.tile("junka", [128, 512], F32)
            sm = kb.pool("sm", [128, 64], F32, 2)
            lnb = kb.pool("lnb", [128, 512], BF16, 2)
            lnT = kb.pool("lnT", [128, 4, 128], BF16, 2)
            hf = kb.pool("hf", [128, 16, 192], F32, 4)
            hbq = kb.pool("hbq", [128, 16, 192], BF16, 2)
            rt = kb.pool("rt", [128, 4, 16, 32], F32, 1)
            vbp = kb.pool("vb", [128, 16, 128], BF16, 2)
            tns = kb.pool("tns", [128, 16, 128], BF16, 2)
            trs = kb.pool("trs", [64, 16, 128], BF16, 2)
            psg = kb.pool("psg", [128, 512], F32, 4, "psum")
            pstb = kb.pool("pstb", [128, 8, 128], BF16, 2, "psum")

            def rstd_small(dst, src, n):
                op("act", lambda e: e.activation(out=dst, in_=src, func=AF.Ln, bias=epsc[:], scale=1.0 / n), [s_, epsc], [s_], tiny=True)
                op("act", lambda e: e.activation(out=dst, in_=dst, func=AF.Exp, scale=-0.5), [s_], [s_], tiny=True)

            def latent_norm_T(lat, c0, woff, s_):
                op("act", lambda e: e.activation(out=junka[:], in_=lat[:, c0:c0 + 512], func=AF.Square), [lat], [junka])
                op("dve", lambda e: e.tensor_reduce(out=s_[:, 0:1], in_=junka[:], axis=AX.X, op=ALU.add), [junka], [s_], tiny=True)
                op("act", lambda e: e.activation(out=s_[:, 1:2], in_=s_[:, 0:1], func=AF.Ln, bias=epsc[:], scale=1.0 / 512), [s_, epsc], [s_], tiny=True)
                op("act", lambda e: e.activation(out=s_[:, 1:2], in_=s_[:, 1:2], func=AF.Exp, scale=-0.5), [s_], [s_], tiny=True)
                lb = lnb.next()
                op("dve", lambda e: e.scalar_tensor_tensor(out=lb[:], in0=lat[:, c0:c0 + 512], scalar=s_[:, 1:2], in1=rv[:, woff:woff + 512],
                                                           op0=ALU.mult, op1=ALU.mult), [lat, s_, rv], [lb])
                pt = pstb.next()
                for j in range(4):
                    op("pe", lambda e, j=j: e.transpose(out=pt[:, j, :], in_=lb[:, j * 128:(j + 1) * 128], identity=identb[:]), [lb, identb], [pt])
                lt = lnT.next()
                op("dve", lambda e: e.tensor_copy(out=lt[:], in_=pt[:, 0:4, :]), [pt], [lt])
                return lt

            def head_post(h_, woff, scale, tt, dTn, dTr, s_):
                op("act", lambda e: e.activation(out=junk[:], in_=h_[:].rearrange("p h d -> p (h d)"), func=AF.Square), [h_], [junk])
                op("dve", lambda e: e.tensor_reduce(out=s_[:, 16:32], in_=junk[:].rearrange("p (h d) -> p h d", d=192), axis=AX.X, op=ALU.add), [junk], [s_], tiny=True)
                op("act", lambda e: e.activation(out=s_[:, 32:48], in_=s_[:, 16:32], func=AF.Ln, bias=epsc[:], scale=1.0 / 192), [s_, epsc], [s_], tiny=True)
                op("act", lambda e: e.activation(out=s_[:, 32:48], in_=s_[:, 32:48], func=AF.Exp, scale=-0.5), [s_], [s_], tiny=True)
                if scale != 1.0:
                    op("dve", lambda e: e.tensor_scalar(out=s_[:, 32:48], in0=s_[:, 32:48], scalar1=scale, scalar2=None, op0=ALU.mult), [s_], [s_], tiny=True)
                for hd in range(16):
                    op("dve", lambda e, hd=hd: e.scalar_tensor_tensor(out=h_[:, hd, :], in0=h_[:, hd, :], scalar=s_[:, 32 + hd:33 + hd], in1=rv[:, woff:woff + 192],
                                                                      op0=ALU.mult, op1=ALU.mult), [h_, s_, rv], [h_], tiny=True)
                chk('pm2h1')
                hb_ = hbq.next()
                op("act", lambda e: e.activation(out=hb_[:, :, 0:128], in_=h_[:, :, 0:128], func=AF.Copy), [h_], [hb_])
                r_ = rt.next()
                x1 = h_[:, :, 128:160]
                x2 = h_[:, :, 160:192]
                cosb = cs[:, tt, 0:32].unsqueeze(1).to_broadcast([128, 16, 32])
                sinb = cs[:, tt, 32:64].unsqueeze(1).to_broadcast([128, 16, 32])
                op("dve", lambda e: e.tensor_tensor(out=r_[:, 0], in0=x1, in1=cosb, op=ALU.mult), [h_, cs], [r_])
                op("dve", lambda e: e.tensor_tensor(out=r_[:, 1], in0=x2, in1=sinb, op=ALU.mult), [h_, cs], [r_])
                op("pool", lambda e: e.tensor_tensor(out=r_[:, 2], in0=x2, in1=cosb, op=ALU.mult), [h_, cs], [r_])
                op("pool", lambda e: e.tensor_tensor(out=r_[:, 3], in0=x1, in1=sinb, op=ALU.mult), [h_, cs], [r_])
                op("dve", lambda e: e.tensor_tensor(out=hb_[:, :, 128:160], in0=r_[:, 0], in1=r_[:, 1], op=ALU.subtract), [r_], [hb_])
                op("dve", lambda e: e.tensor_tensor(out=hb_[:, :, 160:192], in0=r_[:, 2], in1=r_[:, 3], op=ALU.add), [r_], [hb_])
                chk('pm2h2')
                tn = tns.next()
                tr = trs.next()
                for h0 in (0, 8):
                    pt = pstb.next()
                    for j in range(8):
                        op("pe", lambda e, j=j: e.transpose(out=pt[:, j, :], in_=hb_[:, h0 + j, 0:128], identity=identb[:]), [hb_, identb], [pt])
                    op("act" if h0 == 0 else "dve", lambda e: (e.activation(out=tn[:, h0:h0 + 8, :], in_=pt[:], func=AF.Copy) if h0 == 0 else
                                                              e.tensor_copy(out=tn[:, h0:h0 + 8, :], in_=pt[:])), [pt], [tn])
                    pt2 = pstb.next()
                    for j in range(8):
                        op("pe", lambda e, j=j: e.transpose(out=pt2[:64, j, :], in_=hb_[:, h0 + j, 128:192], identity=identb[:]), [hb_, identb], [pt2])
                    op("dve", lambda e: e.tensor_copy(out=tr[:, h0:h0 + 8, :], in_=pt2[:64, :, :]), [pt2], [tr])
                chk('pm2h3')
                tsl = slice(tt * 128, (tt + 1) * 128)
                for h0 in range(0, 16, 4):
                    kb.dma("sp", dTn[h0:h0 + 4, :, tsl].rearrange("h d t -> d h t"), tn[:, h0:h0 + 4, :], tn, False, first=(h0 == 0))
                for h0 in range(0, 16, 8):
                    kb.dma("sp", dTr[h0:h0 + 8, :, tsl].rearrange("h d t -> d h t"), tr[:, h0:h0 + 8, :], tr, False, first=(h0 == 0))

            def pm2_a(tt):
                rows = slice(tt * 128, (tt + 1) * 128)
                lat = latp.next()
                kb.dma("sp", lat[:], lat_tm[rows, :], lat, True)
                s_ = sm.next()
                lt = latent_norm_T(lat, 0, R_QNW, s_)
                qf = hf.next()
                for cbk in range(6):
                    ps = psg.next()
                    for kc in range(4):
                        op("pe", lambda e, ps=ps, kc=kc, cbk=cbk: e.matmul(ps[:], lhsT=lt[:, kc, :], rhs=wq[:, kc, cbk * 512:(cbk + 1) * 512],
                                                                         start=(kc == 0), stop=(kc == 3)), [lt, wq], [ps])
                    op("act" if cbk % 2 == 0 else "dve",
                       lambda e, ps=ps, cbk=cbk: (e.activation(out=qf[:].rearrange("p h d -> p (h d)")[:, cbk * 512:(cbk + 1) * 512], in_=ps[:], func=AF.Copy)
                                                  if cbk % 2 == 0 else
                                                  e.tensor_copy(out=qf[:].rearrange("p h d -> p (h d)")[:, cbk * 512:(cbk + 1) * 512], in_=ps[:])), [ps], [qf])
                yield
                lt2 = latent_norm_T(lat, 512, R_KVNW, s_)
                kf_ = hf.next()
                vb = vbp.next()
                for cbk in range(8):
                    ps = psg.next()
                    for kc in range(4):
                        op("pe", lambda e, ps=ps, kc=kc, cbk=cbk: e.matmul(ps[:], lhsT=lt2[:, kc, :], rhs=wkv[:, kc, cbk * 512:(cbk + 1) * 512],
                                                                         start=(kc == 0), stop=(kc == 3)), [lt2, wkv], [ps])
                    psv = ps[:].rearrange("p (h c) -> p h c", c=256)
                    op("act", lambda e, psv=psv, cbk=cbk: e.activation(out=kf_[:, 2 * cbk:2 * cbk + 2, 0:128], in_=psv[:, :, 0:128], func=AF.Copy), [ps], [kf_])
                    op("dve", lambda e, psv=psv, cbk=cbk: e.tensor_copy(out=vb[:, 2 * cbk:2 * cbk + 2, :], in_=psv[:, :, 128:256]), [ps], [vb])
                op("dve", lambda e: e.tensor_copy(out=kf_[:, :, 128:192], in_=lat[:, 1024:1088].unsqueeze(1).to_broadcast([128, 16, 64])), [lat], [kf_])
                for h0 in range(0, 16, 4):
                    kb.dma("sp", V_tm[h0:h0 + 4, :, tt * 128:(tt + 1) * 128].rearrange("h p c -> p h c"), vb[:, h0:h0 + 4, :], vb, False, first=(h0 == 0))
                res_a[tt] = (qf, kf_, s_)
                yield

            def pm2_b(tt, ctx):
                qf, kf_, s_ = ctx
                head_post(qf, R_QKQ, float(192 ** -0.5), tt, qTn, qTr, s_)
                yield
                head_post(kf_, R_QKK, 1.0, tt, kTn, kTr, s_)
                yield

            res_a = {}
            ga = pm2_a(0)
            for _ in ga:
                pass
            for tt in range(NT):
                ga = pm2_a(tt + 1) if tt + 1 < NT else iter(())
                gb = pm2_b(tt, res_a[tt])
                next(ga, None)
                next(gb, None)
                for _ in ga:
                    pass
                for _ in gb:
                    pass
            kb.pop()

            chk('pm2')
            kb.push()
            ktn = kb.pool("ktn", [128, T], BF16, 2)
            ktr = kb.pool("ktr", [128, T], BF16, 2)
            qtn = kb.pool("qtn", [128, T], BF16, 2)
            qtr = kb.pool("qtr", [128, T], BF16, 2)
            vh = kb.pool("vh", [128, NT, 128], BF16, 2)
            ptp = kb.pool("pT", [128, 512], BF16, 3)
            recp = kb.pool("rec", [128, 512], F32, 2)
            obp = kb.pool("ob", [128, 512], BF16, 2)
            pssc = kb.pool("pssc", [128, 512], F32, 4, "psum")
            pso = kb.pool("pso", [128, 512], F32, 2, "psum")
            pss_ = kb.pool("pssum", [128, 512], F32, 2, "psum")
            for b_ in ktr.bufs + qtr.bufs:
                op("pool", lambda e, b_=b_: e.memset(b_[64:128, :], 0.0), (), [b_])
            for h in range(16):
                kn = ktn.next(); kr = ktr.next(); qn = qtn.next(); qr = qtr.next(); v_ = vh.next()
                kb.dma("sp", kn[:], kTn[h], kn, True)
                kb.dma("sp", kr[0:64, :], kTr[h], kr, True)
                kb.dma("sp", qn[:], qTn[h], qn, True)
                kb.dma("sp", qr[0:64, :], qTr[h], qr, True)
                kb.dma("sp", v_[:].rearrange("p j c -> p (j c)"), V_tm[h], v_, True)
                for I in range(NTB):
                    po = pso.next()
                    pm = pss_.next()
                    nj = 4 * I + 4
                    def scores(j):
                        a = j - 4 * I if j >= 4 * I else 0
                        c0 = 128 * a
                        q0 = I * 512 + c0
                        q1 = (I + 1) * 512
                        sc = pssc.next()
                        op("pe", lambda e: e.matmul(sc[:, c0:512], lhsT=kn[:, j * 128:(j + 1) * 128], rhs=qn[:, q0:q1], start=True, stop=False), [kn, qn], [sc])
                        op("pe", lambda e: e.matmul(sc[:, c0:512], lhsT=kr[:, j * 128:(j + 1) * 128], rhs=qr[:, q0:q1], start=False, stop=True), [kr, qr], [sc])
                        return sc, c0

                    def rest(j, sc, c0):
                        pT = ptp.next()
                        op("act", lambda e: e.activation(out=pT[:, c0:512], in_=sc[:, c0:512], func=AF.Exp), [sc], [pT])
                        if j >= 4 * I:
                            op("dve", lambda e: e.tensor_tensor(out=pT[:, c0:c0 + 128], in0=pT[:, c0:c0 + 128], in1=dmask[:], op=ALU.mult), [pT, dmask], [pT])
                        op("pe", lambda e: e.matmul(po[:, c0:512], lhsT=v_[:, j, :], rhs=pT[:, c0:512], start=(j == 0), stop=(j == nj - 1)), [v_, pT], [po])
                        op("pe", lambda e: e.matmul(pm[:, c0:512], lhsT=onesb[:], rhs=pT[:, c0:512], start=(j == 0), stop=(j == nj - 1)), [onesb, pT], [pm])

                    pend = [scores(0)]
                    if nj > 1:
                        pend.append(scores(1))
                    for j in range(nj):
                        if j + 2 < nj:
                            pend.append(scores(j + 2))
                        rest(j, *pend.pop(0))
                    rec = recp.next()
                    op("dve", lambda e: e.reciprocal(out=rec[:], in_=pm[:]), [pm], [rec])
                    ob = obp.next()
                    op("dve", lambda e: e.tensor_tensor(out=ob[:], in0=po[:], in1=rec[:], op=ALU.mult), [po, rec], [ob])
                    kb.dma("sp", oT[h * 128:(h + 1) * 128, I * 512:(I + 1) * 512], ob[:], ob, False)
            kb.pop()

            chk('pm3')
            kb.push()
            S = [kb.tile("S%d" % g, [128, 512], F32) for g in range(NG)]
            Sb = [kb.tile("Sb%d" % g, [128, 512], BF16) for g in range(NG)]
            for g in range(NG):
                op("pool", lambda e, g=g: e.memset(S[g][:], 0.0), (), [S[g]])
                op("pool", lambda e, g=g: e.memset(Sb[g][:], 0.0), (), [Sb[g]])
            xcp = kb.pool("xc", [128, DI], BF16, 2)
            zcp = kb.pool("zc", [128, DI], BF16, 1)
            btmp = kb.pool("btm", [128, 1024], BF16, 2)
            btp = kb.pool("bt", [128, NG, 128], BF16, 2)
            ctp = kb.pool("ct", [128, NG, 128], BF16, 2)
            a2p = kb.pool("a2s", [128, 128], F32, 2)
            abg = [kb.tile("ab%d" % g, [128, 8, 128], F32) for g in range(NG)]
            Eg = [[kb.tile("E%d_%d" % (i, g), [128, 8, 128], BF16) for g in range(NG)] for i in range(2)]
            xdg = [[kb.tile("xd%d_%d" % (i, g), [128, 512], BF16) for g in range(NG)] for i in range(2)]
            cbp2 = kb.pool("cbb", [128, NG, 128], BF16, 1)
            elp = kb.pool("el", [128, 64], F32, 2)
            eap = kb.pool("ea", [128, 64], F32, 2)
            ytp = kb.pool("yt", [128, 512], F32, 2)
            y2p = kb.pool("y2", [128, 512], F32, 2)
            sqp = kb.pool("sq", [128, 512], F32, 1)
            gqg = [kb.tile("gq%d" % g, [128, 512], F32) for g in range(NG)]
            gbp = kb.pool("gb", [128, DI], BF16, 1)
            ssp = kb.pool("ssg", [128, 16], F32, 2)
            gtp = kb.pool("gts", [128, 32, 128], BF16, 1)
            pscb = kb.pool("pscb", [128, 4, 128], F32, 2, "psum")
            psyd = kb.pool("psyd", [128, 512], F32, 2, "psum")
            psyo = kb.pool("psyo", [128, 512], F32, 2, "psum")
            psst = kb.pool("psst", [128, 512], F32, 1, "psum")
            pstr = kb.pool("pstr", [128, 8, 128], BF16, 1, "psum")

            def ssd_front(c):
                rows = slice(c * 128, (c + 1) * 128)
                par = c % 2
                xc = xcp.next(); btm = btmp.next(); bt = btp.next(); ct_ = ctp.next(); a2_ = a2p.next()
                kb.dma("sp", a2_[:], acs2[rows, :], a2_, True)
                for g0 in (0, 4):
                    kb.dma("sp", bt[:, g0:g0 + 4, :], BT[g0 * 128:(g0 + 4) * 128, rows].rearrange("(g n) k -> n g k", n=128), bt, True, first=(g0 == 0))
                    kb.dma("sp", ct_[:, g0:g0 + 4, :], CT[g0 * 128:(g0 + 4) * 128, rows].rearrange("(g n) k -> n g k", n=128), ct_, True, first=(g0 == 0))
                kb.dma("sp", xc[:], x_tm[rows, :], xc, True)
                kb.dma("sp", btm[:], B_tm[rows, :], btm, True)
                el = elp.next(); ea = eap.next()
                op("act", lambda e: e.activation(out=ea[:], in_=a2_[:, 64:128], func=AF.Exp), [a2_], [ea])
                cbb = cbp2.next()
                for half in range(2):
                    pc = pscb.next()
                    for gg in range(4):
                        g = half * 4 + gg
                        op("pe", lambda e, pc=pc, g=g, gg=gg: e.matmul(pc[:, gg, :], lhsT=bt[:, g, :], rhs=ct_[:, g, :], start=True, stop=True), [bt, ct_], [pc])
                    op("act", lambda e, pc=pc, half=half: e.activation(out=cbb[:, half * 4:half * 4 + 4, :], in_=pc[:], func=AF.Copy), [pc], [cbb])
                for g in range(NG):
                    ab = abg[g]
                    kb.dma("sp", ab[:].rearrange("p r q -> p (r q)"), acsB[c, g * 1024:(g + 1) * 1024].partition_broadcast(128), ab, True)
                for g in range(NG):
                    ab = abg[g]
                    hs = slice(g * 8, (g + 1) * 8)
                    op("act", lambda e, ab=ab, hs=hs: e.activation(out=el[:, hs], in_=ab[:, :, 127], func=AF.Exp), [ab], [el], tiny=True)
                    op("dve", lambda e, ab=ab, hs=hs: e.tensor_tensor(out=ab[:], in0=ab[:], in1=a2_[:, hs].unsqueeze(2).to_broadcast([128, 8, 128]), op=ALU.subtract), [ab, a2_], [ab])
                for g in range(NG):
                    ab = abg[g]
                    op("dve" if g % 2 == 0 else "pool", lambda e, ab=ab: e.tensor_tensor(out=ab[:], in0=ab[:], in1=mneg[:].unsqueeze(1).to_broadcast([128, 8, 128]), op=ALU.add), [ab, mneg], [ab])
                for g in range(NG):
                    ab = abg[g]
                    E_ = Eg[par][g]
                    op("act", lambda e, ab=ab, E_=E_: e.activation(out=E_[:], in_=ab[:], func=AF.Exp), [ab], [E_])
                for g in range(NG):
                    E_ = Eg[par][g]
                    xd = xdg[par][g]
                    op("pool", lambda e, E_=E_, xd=xd, g=g: e.tensor_tensor(out=xd[:].rearrange("p (r d) -> p r d", d=64), in0=xc[:, g * 512:(g + 1) * 512].rearrange("p (r d) -> p r d", d=64),
                                                                        in1=E_[:, :, 127].unsqueeze(2).to_broadcast([128, 8, 64]), op=ALU.mult), [xc, E_], [xd])
                for g in range(NG):
                    E_ = Eg[par][g]
                    op("dve", lambda e, E_=E_, g=g: e.tensor_tensor(out=E_[:], in0=E_[:], in1=cbb[:, g, :].unsqueeze(1).to_broadcast([128, 8, 128]), op=ALU.mult), [E_, cbb], [E_])
                return (rows, par, xc, btm, ct_, ea, el)

            def ssd_back(c, ctx):
                rows, par, xc, btm, ct_, ea, el = ctx
                zc = zcp.next()
                kb.dma("sp", zc[:], zs_tm[rows, :], zc, True)
                ssg = ssp.next()
                stA = {}

                def st_a(g):
                    E_ = Eg[par][g]
                    pyd = psyd.next()
                    for r in range(8):
                        hh = g * 8 + r
                        op("pe", lambda e, pyd=pyd, hh=hh, r=r, E_=E_: e.matmul(pyd[:, r * 64:(r + 1) * 64], lhsT=E_[:, r, :], rhs=xc[:, hh * 64:(hh + 1) * 64],
                                                                                start=True, stop=True), [E_, xc], [pyd])
                    pyo = psyo.next()
                    op("pe", lambda e, pyo=pyo, g=g: e.matmul(pyo[:], lhsT=ct_[:, g, :], rhs=Sb[g][:], start=True, stop=True), [ct_, Sb[g]], [pyo])
                    yt = ytp.next()
                    op("dve", lambda e, pyo=pyo, g=g, yt=yt: e.tensor_tensor(out=yt[:].rearrange("p (r d) -> p r d", d=64), in0=pyo[:].rearrange("p (r d) -> p r d", d=64),
                                                                             in1=ea[:, g * 8:(g + 1) * 8].unsqueeze(2).to_broadcast([128, 8, 64]), op=ALU.mult), [pyo, ea], [yt])
                    op("dve", lambda e, pyd=pyd, yt=yt: e.tensor_tensor(out=yt[:], in0=yt[:], in1=pyd[:], op=ALU.add), [pyd, yt], [yt])
                    stA[g] = yt

                st_a(0)
                for g in range(NG):
                    gs = slice(g * 512, (g + 1) * 512)
                    xd = xdg[par][g]
                    gq = gqg[g]
                    if g + 1 < NG:
                        st_a(g + 1)
                    yt = stA[g]
                    y2 = y2p.next()
                    op("pool", lambda e, g=g, y2=y2, gs=gs: e.tensor_tensor(out=y2[:].rearrange("p (r d) -> p r d", d=64), in0=xc[:, gs].rearrange("p (r d) -> p r d", d=64),
                                                                            in1=rv[:, R_D + g * 8:R_D + (g + 1) * 8].unsqueeze(2).to_broadcast([128, 8, 64]), op=ALU.mult), [xc, rv], [y2])
                    op("pool", lambda e, y2=y2, yt=yt: e.tensor_tensor(out=y2[:], in0=y2[:], in1=yt[:], op=ALU.add), [y2, yt], [y2])
                    op("pool", lambda e, y2=y2, gs=gs, gq=gq: e.tensor_tensor(out=gq[:], in0=y2[:], in1=zc[:, gs], op=ALU.mult), [y2, zc], [gq])
                    sq_ = sqp.next()
                    op("act", lambda e, sq_=sq_, gq=gq: e.activation(out=sq_[:], in_=gq[:], func=AF.Square), [gq], [sq_])
                    pst_ = psst.next()
                    op("pe", lambda e, pst_=pst_, g=g, xd=xd: e.matmul(pst_[:], lhsT=btm[:, g * 128:(g + 1) * 128], rhs=xd[:], start=True, stop=True), [btm, xd], [pst_])
                    op("dve", lambda e, g=g: e.tensor_tensor(out=S[g][:].rearrange("p (r d) -> p r d", d=64), in0=S[g][:].rearrange("p (r d) -> p r d", d=64),
                                                             in1=el[:, g * 8:(g + 1) * 8].unsqueeze(2).to_broadcast([128, 8, 64]), op=ALU.mult), [S[g], el], [S[g]])
                    op("dve", lambda e, g=g, pst_=pst_: e.tensor_tensor(out=S[g][:], in0=S[g][:], in1=pst_[:], op=ALU.add), [S[g], pst_], [S[g]])
                    op("act", lambda e, g=g: e.activation(out=Sb[g][:], in_=S[g][:], func=AF.Copy), [S[g]], [Sb[g]])
                    op("dve", lambda e, sq_=sq_, g=g: e.tensor_reduce(out=ssg[:, g:g + 1], in_=sq_[:], axis=AX.X, op=ALU.add), [sq_], [ssg], tiny=True)
                op("dve", lambda e: e.tensor_reduce(out=ssg[:, 8:9], in_=ssg[:, 0:8], axis=AX.X, op=ALU.add), [ssg], [ssg], tiny=True)
                op("act", lambda e: e.activation(out=ssg[:, 9:10], in_=ssg[:, 8:9], func=AF.Ln, bias=epsc[:], scale=1.0 / DI), [ssg, epsc], [ssg], tiny=True)
                op("act", lambda e: e.activation(out=ssg[:, 9:10], in_=ssg[:, 9:10], func=AF.Exp, scale=-0.5), [ssg], [ssg], tiny=True)
                gb_ = gbp.next()
                for g in range(NG):
                    op("act", lambda e, g=g: e.activation(out=gb_[:, g * 512:(g + 1) * 512], in_=gqg[g][:], func=AF.Copy, scale=ssg[:, 9:10]), [gqg[g], ssg], [gb_])
                gts = gtp.next()
                for j0 in range(0, 32, 8):
                    pt = pstr.next()
                    for j in range(8):
                        op("pe", lambda e, pt=pt, j=j, j0=j0: e.transpose(out=pt[:, j, :], in_=gb_[:, (j0 + j) * 128:(j0 + j + 1) * 128], identity=identb[:]), [gb_, identb], [pt])
                    op("dve", lambda e, pt=pt, j0=j0: e.tensor_tensor(out=gts[:, j0:j0 + 8, :], in0=pt[:],
                                                                      in1=vecs[:, V_SNW + j0:V_SNW + j0 + 8].unsqueeze(2).to_broadcast([128, 8, 128]), op=ALU.mult), [pt, vecs], [gts])
                for j0 in range(0, 32, 4):
                    kb.dma("sp", gT[j0 * 128:(j0 + 4) * 128, rows].rearrange("(j p) t -> p j t", p=128), gts[:, j0:j0 + 4, :], gts, False, first=(j0 == 0))

            ctx = ssd_front(0)
            for c in range(NT):
                nxt = ssd_front(c + 1) if c + 1 < NT else None
                ssd_back(c, ctx)
                ctx = nxt
            kb.pop()

            chk('p4')
            kb.push()
            gap = kb.pool("ga", [128, 512], BF16, 3)
            gbp2 = kb.pool("gb2", [128, 512], BF16, 3)
            m1p = kb.pool("m1", [128, 512], F32, 2)
            m2p = kb.pool("m2", [128, 512], F32, 2)
            mbp = kb.pool("mb", [128, 512], BF16, 2)

            def pre5(ji, t0):
                sl = slice(t0, t0 + 512)
                rs_ = slice(ji * 128, (ji + 1) * 128)
                ga_ = gap.next(); gb3 = gbp2.next()
                kb.dma("sp", ga_[:], sga[rs_, sl], ga_, True)
                kb.dma("sp", gb3[:], sgb[rs_, sl], gb3, True)
                return ga_, gb3

            def epi5(ji, t0, pss, pctx):
                sl = slice(t0, t0 + 512)
                rs_ = slice(ji * 128, (ji + 1) * 128)
                ga_, gb3 = pctx
                m1 = m1p.next(); m2 = m2p.next(); mb = mbp.next()
                op("dve", lambda e: e.tensor_tensor(out=m1[:], in0=pss[0][:], in1=ga_[:], op=ALU.mult), [pss[0], ga_], [m1])
                op("dve", lambda e: e.tensor_tensor(out=m2[:], in0=pss[1][:], in1=gb3[:], op=ALU.mult), [pss[1], gb3], [m2])
                op("pool", lambda e: e.tensor_tensor(out=mb[:], in0=m1[:], in1=m2[:], op=ALU.add), [m1, m2], [mb])
                kb.dma("sp", mergedT[rs_, sl], mb[:], mb, False)

            gemm_fm(1024, [(gT, 32), (oT, KD)], [[(0, wsso_in[l, i], 128), (1, wmo_in[l, i], 128)] for i in range(16)], epi5, pre5)
            kb.pop()

            chk('p5')
            def resid_phase(act_d, KC, w_in_, gsel, TBLK):
                kb.push()
                xrp = kb.pool("xr", [128, 512], F32, 4)

                def pre6(ji, t0):
                    xr = xrp.next()
                    kb.dma("sp", xr[:], xT[ji * 128:(ji + 1) * 128, t0:t0 + 512], xr, True)
                    return xr

                def epi6(ji, t0, pss, xr):
                    sl = slice(t0, t0 + 512)
                    rs_ = slice(ji * 128, (ji + 1) * 128)
                    op("dve", lambda e: e.scalar_tensor_tensor(out=xr[:], in0=pss[0][:], scalar=mv[:, gsel, ji:ji + 1], in1=xr[:], op0=ALU.mult, op1=ALU.add),
                       [pss[0], mv, xr], [xr])
                    kb.dma("sp", xT[rs_, sl], xr[:], xr, False)

                gemm_fm(TBLK, [(act_d, KC)], [[(0, w_in_[l, i], 128)] for i in range(16)], epi6, pre6)
                kb.pop()

            resid_phase(mergedT, KD, wmix_in, 2, T)

            chk('p6')
            norm_phase(lambda kc: mv[:, 3, kc:kc + 1], lambda kc: mv[:, 4, kc:kc + 1], mv)

            chk('p7')
            kb.push()
            fcb = kb.pool("fcb", [128, 514], F32, 2)
            facc = kb.pool("facc", [128, 512], F32, 2)
            fsg = kb.pool("fsg", [128, 512], F32, 2)
            fab = kb.pool("fab", [128, 512], BF16, 3)

            def epi8(ji, t0, pss, pctx=None):
                sl = slice(t0, t0 + 512)
                cb = fcb.next()
                if t0 == 0:
                    op("pool", lambda e: e.memset(cb[:, 0:2], 0.0), (), [cb])
                else:
                    pv = fcb.prev()
                    op("pool", lambda e: e.tensor_copy(out=cb[:, 0:2], in_=pv[:, 512:514]), [pv], [cb])
                op("act", lambda e: e.activation(out=cb[:, 2:514], in_=pss[0][:], func=AF.Copy), [pss[0]], [cb])
                ac = facc.next()
                fw = lambda k: vecs[:, V_FW + k * 44 + ji:V_FW + k * 44 + ji + 1]
                op("dve", lambda e: e.tensor_scalar(out=ac[:], in0=cb[:, 0:512], scalar1=fw(0), scalar2=vecs[:, V_FB + ji:V_FB + ji + 1],
                                                    op0=ALU.mult, op1=ALU.add), [cb, vecs], [ac])
                for k in (1, 2):
                    op("dve", lambda e, k=k: e.scalar_tensor_tensor(out=ac[:], in0=cb[:, k:k + 512], scalar=fw(k), in1=ac[:], op0=ALU.mult, op1=ALU.add), [cb, vecs, ac], [ac])
                sg = fsg.next()
                op("act", lambda e: e.activation(out=sg[:], in_=ac[:], func=AF.Silu), [ac], [sg])
                ab_ = fab.next()
                op("dve", lambda e: e.tensor_tensor(out=ab_[:], in0=sg[:], in1=pss[1][:], op=ALU.mult), [sg, pss[1]], [ab_])
                kb.dma("sp", actT[ji * 128:(ji + 1) * 128, sl], ab_[:], ab_, False)

            gemm_fm(T, [(hT, KD)], [[(0, wup_in[l, i], 128), (0, wup_in[l, 44 + i], 128)] for i in range(44)], epi8)
            kb.pop()

            chk('p8')
            resid_phase(actT, KF, wdn_in, 5, 1024)

            kb.pop()
    except _Stop:
        pass
    return nc


def _tile_fm(w, KC):
    K, M = w.shape
    return np.ascontiguousarray(w.reshape(KC, 128, M // 128, 128).transpose(2, 1, 0, 3))


def _tile_tm(w, KC, N):
    K, M = w.shape
    return np.ascontiguousarray(w.reshape(KC, 128, M // N, N).transpose(2, 1, 0, 3))


def _col(v):
    return np.ascontiguousarray(np.asarray(v, np.float32).reshape(-1, 128).T)


def prep_shared(inp):
    f = lambda k: np.asarray(inp[k], np.float32)
    w_in = f("w_in")
    sh = {}
    sh["wmod"] = np.ascontiguousarray(f("w_mod").reshape(L, KD, 128, 6 * D))
    vecs = np.zeros((L, 128, NV), np.float32)
    rv = np.zeros((L, NR), np.float32)
    wa, wb, wb2 = [], [], []
    for l in range(L):
        vecs[l, :, V_BMOD:V_BMOD + 96] = _col(f("b_mod")[l])
        vecs[l, :, V_N1:V_N1 + 16] = _col(f("norm1_w")[l])
        vecs[l, :, V_N2:V_N2 + 16] = _col(f("norm2_w")[l])
        for k in range(4):
            vecs[l, :, V_CW + k * 48:V_CW + (k + 1) * 48] = _col(f("ssm_conv_w")[l, k])
        vecs[l, :, V_CB:V_CB + 48] = _col(f("ssm_conv_b")[l])
        vecs[l, :, V_SNW:V_SNW + 32] = _col(f("ssm_norm_w")[l])
        for k in range(3):
            vecs[l, :, V_FW + k * 44:V_FW + (k + 1) * 44] = _col(f("ffn_conv_w")[l, k])
        vecs[l, :, V_FB:V_FB + 44] = _col(f("ffn_conv_b")[l])
        rv[l, R_DTB:R_DTB + 64] = f("ssm_dt_bias")[l]
        rv[l, R_ALOG:R_ALOG + 64] = f("ssm_a_log")[l]
        rv[l, R_D:R_D + 64] = f("ssm_d")[l]
        rv[l, R_QNW:R_QNW + 512] = f("mla_q_norm_w")[l]
        rv[l, R_KVNW:R_KVNW + 512] = f("mla_kv_norm_w")[l]
        rv[l, R_QKQ:R_QKQ + 192] = f("qk_norm_q_w")[l]
        rv[l, R_QKK:R_QKK + 192] = f("qk_norm_k_w")[l]
        w = w_in[l]
        z, xbc, dt_, ql, kvl, kr, ga, gb = (w[:, 0:4096], w[:, 4096:10240], w[:, 10240:10304], w[:, 10304:10816],
                                            w[:, 10816:11328], w[:, 11328:11392], w[:, 11392:13440], w[:, 13440:15488])
        wa.append(_tile_fm(np.concatenate([xbc, ga, gb], axis=1), KD))
        wb.append(_tile_tm(np.concatenate([z, ql, kvl], axis=1), KD, 512))
        wb2.append(_tile_tm(np.concatenate([dt_, kr], axis=1), KD, 128)[0])
    sh["vecs"] = vecs
    sh["rv"] = rv
    sh["wa"] = np.stack(wa)
    sh["wb"] = np.stack(wb)
    sh["wb2"] = np.stack(wb2)
    sh["wqup"] = np.ascontiguousarray(f("w_q_up").reshape(L, 4, 128, 3072).transpose(0, 2, 1, 3))
    sh["wkvup"] = np.ascontiguousarray(f("w_kv_up").reshape(L, 4, 128, 4096).transpose(0, 2, 1, 3))
    sh["wsso"] = np.stack([_tile_fm(f("w_ssm_out")[l], 32) for l in range(L)])
    sh["wmo"] = np.stack([_tile_fm(f("w_mla_out")[l], 16) for l in range(L)])
    sh["wmix"] = np.stack([_tile_fm(f("w_mix_out")[l], 16) for l in range(L)])
    sh["wup"] = np.stack([_tile_fm(f("w_ffn_up")[l], KD) for l in range(L)])
    sh["wdn"] = np.stack([_tile_fm(f("w_ffn_down")[l], KF) for l in range(L)])
    kk = np.arange(128)
    sh["ident"] = np.eye(128, dtype=np.float32)
    sh["triu"] = (kk[:, None] <= kk[None, :]).astype(np.float32)
    sh["mneg"] = np.where(kk[:, None] <= kk[None, :], 0.0, -30000.0).astype(np.float32)
    sh["dmask"] = ((kk[:, None] < 64) | (kk[None, :] >= 64)).astype(np.float32)
    sh["invf"] = (np.float32(10000.0) ** (-np.arange(0, 64, 2, dtype=np.float32) / np.float32(64))).astype(np.float32)
    return sh


def prep_core(inp, b, T):
    NT = T // 128
    return {
        "xT": np.ascontiguousarray(np.asarray(inp["x"][b, :T], np.float32).T),
        "cT": _col(np.asarray(inp["c"][b], np.float32)),
        "pos": np.ascontiguousarray(np.asarray(inp["positions"][b, :T], np.int32).reshape(NT, 128).T),
    }


_NC_CACHE = {}


def run(inp, T, batches, dbg=(), stop=None):
    key = (T, tuple(dbg), stop)
    if key not in _NC_CACHE:
        _NC_CACHE[key] = build(T, dbg, stop)
    nc = _NC_CACHE[key]
    sh = prep_shared(inp)
    in_maps = []
    for b in batches:
        m = dict(sh)
        m.update(prep_core(inp, b, T))
        in_maps.append(m)
    res = run_bass_kernel_spmd(nc, in_maps, core_ids=list(range(len(batches))))
    return res.results


def kernel(**inputs):
    B, T = inputs["x"].shape[0], inputs["x"].shape[1]
    results = run(inputs, T, list(range(B)))
    out = np.stack([np.ascontiguousarray(r["outT"].T) for r in results], axis=0)
    return out.astype(np.float32)
```
